# Optimizing a Trainium2 kernel written in Bass

```python
import jax, jax.numpy as jnp
from jax import lax
import numpy as np

D_MODEL = 1024
BATCH = 8
SEQ = 2048
DEPTH = 4
DEC_BATCH = 128
DEC_SEQ = 4
PAST_LEN = 16384
PAGE_SIZE = 128

RET_HEADS = 4
RET_DK = 128
RET_DV = 128
D_RET = RET_HEADS * RET_DK
D_RET_V = RET_HEADS * RET_DV
RET_CHUNK = 128
CONV_CH = 512
CONV_W = 3
D_FF = 2816
PLE_DIM = 256
N_NORMS = 8
SPLITS = [D_RET, D_RET, D_RET_V, D_RET_V, CONV_CH, CONV_CH, CONV_CH, D_MODEL, D_MODEL]
N_IN = 2 * D_RET + 2 * D_RET_V + 3 * CONV_CH + 2 * D_MODEL
ROPE_BASE = 10000.0
EPS = 1e-6

kernel_name = "retention_shortconv_gated_hybrid_step"


def rmsnorm(x, g):
    xf = x.astype(jnp.float32)
    y = xf * lax.rsqrt(jnp.mean(xf * xf, axis=-1, keepdims=True) + EPS) * g.astype(jnp.float32)
    return y.astype(x.dtype)


def swiglu(x, wi, wo):
    gate, up = jnp.split(x @ wi, 2, axis=-1)
    return (jax.nn.silu(gate) * up) @ wo


def rotary(t, pos):
    d = t.shape[-1]
    inv_freq = ROPE_BASE ** (-jnp.arange(0, d, 2, dtype=jnp.float32) / d)
    ang = pos[:, None] * inv_freq[None, :]
    cos = jnp.cos(ang)[None, :, None, :].astype(t.dtype)
    sin = jnp.sin(ang)[None, :, None, :].astype(t.dtype)
    t1, t2 = jnp.split(t, 2, axis=-1)
    return jnp.concatenate([t1 * cos - t2 * sin, t1 * sin + t2 * cos], axis=-1)


def retention(q, k, v, s0):
    B, L, H, DK = q.shape
    DV = v.shape[-1]
    C = RET_CHUNK if L % RET_CHUNK == 0 else L
    N = L // C
    dt = q.dtype
    log_gamma = jnp.log1p(-jnp.exp2(-5.0 - jnp.arange(H, dtype=jnp.float32)))
    idx = jnp.arange(C, dtype=jnp.float32)
    diff = idx[:, None] - idx[None, :]
    dmask = jnp.where(diff[None] >= 0, jnp.exp(log_gamma[:, None, None] * jnp.maximum(diff, 0.0)[None]), 0.0).astype(dt)
    q_dec = jnp.exp(log_gamma[:, None] * (idx[None] + 1.0)).astype(dt)
    k_dec = jnp.exp(log_gamma[:, None] * (C - 1.0 - idx[None])).astype(dt)
    chunk_dec = jnp.exp(log_gamma * C).astype(dt)

    def to_chunks(t):
        return t.reshape(B, N, C, H, t.shape[-1]).transpose(1, 0, 3, 2, 4)

    def step(S, inp):
        qc, kc, vc = inp
        scores = jnp.einsum('bhcd,bhed->bhce', qc, kc) * dmask
        o = (jnp.einsum('bhce,bhev->bhcv', scores, vc)
             + jnp.einsum('bhcd,bhdv->bhcv', qc, S) * q_dec[..., None])
        S_new = chunk_dec[:, None, None] * S + jnp.einsum('bhcd,bhcv->bhdv', kc * k_dec[..., None], vc)
        return S_new.astype(S.dtype), o.astype(dt)

    S, o = lax.scan(step, s0, (to_chunks(q), to_chunks(k), to_chunks(v)))
    o = o.transpose(1, 0, 3, 2, 4).reshape(B, L, H, DV)
    return o, S


def mixer(xn, s_ret, s_conv, pos, w_in, ret_gn, w_ret_out, conv_w, w_conv_out, w_o):
    B, L, _ = xn.shape
    offs = np.cumsum(SPLITS)[:-1].tolist()
    q, k, v, g, cb, cc, ch, ga, gb = jnp.split(xn @ w_in, offs, axis=-1)
    q = rotary(q.reshape(B, L, RET_HEADS, RET_DK), pos)
    k = rotary(k.reshape(B, L, RET_HEADS, RET_DK), pos) * (RET_DK ** -0.5)
    v = v.reshape(B, L, RET_HEADS, RET_DV)
    o, s_ret_new = retention(q, k, v, s_ret)
    of = o.astype(jnp.float32)
    mu = jnp.mean(of, axis=-1, keepdims=True)
    var = jnp.mean((of - mu) ** 2, axis=-1, keepdims=True)
    on = (of - mu) * lax.rsqrt(var + EPS) * ret_gn.reshape(RET_HEADS, RET_DV).astype(jnp.float32)
    on = on.astype(xn.dtype).reshape(B, L, D_RET_V)
    u_a = (jax.nn.silu(g) * on) @ w_ret_out
    cin = cc * ch
    full = jnp.concatenate([s_conv.astype(cin.dtype), cin], axis=1)
    conv = full[:, 0:L] * conv_w[0]
    for j in range(1, CONV_W):
        conv = conv + full[:, j:j + L] * conv_w[j]
    s_conv_new = full[:, -(CONV_W - 1):]
    u_b = (cb * conv) @ w_conv_out
    merged = jax.nn.sigmoid(ga) * u_a + jax.nn.sigmoid(gb) * u_b
    return merged @ w_o, s_ret_new, s_conv_new


def trunk(x, p, s_ret, s_conv, pos, norm_g, w_ffn1_in, w_ffn1_out, w_in, ret_gn, w_ret_out,
          conv_w, w_conv_out, w_o, w_ffn2_in, w_ffn2_out, w_ple_gate, w_ple):
    h = x
    new_r = []
    new_c = []
    for i in range(DEPTH):
        g = norm_g[i]
        h = h + 0.5 * rmsnorm(swiglu(rmsnorm(h, g[0]), w_ffn1_in[i], w_ffn1_out[i]), g[1])
        mix, r, c = mixer(rmsnorm(h, g[2]), s_ret[i], s_conv[i], pos, w_in[i], ret_gn[i],
                          w_ret_out[i], conv_w[i], w_conv_out[i], w_o[i])
        h = h + rmsnorm(mix, g[3])
        h = h + 0.5 * rmsnorm(swiglu(rmsnorm(h, g[4]), w_ffn2_in[i], w_ffn2_out[i]), g[5])
        gate = jax.nn.sigmoid(rmsnorm(h, g[6]) @ w_ple_gate[i])
        h = h + rmsnorm((p[i] @ w_ple[i]) * gate, g[7])
        new_r.append(r)
        new_c.append(c)
    return h, jnp.stack(new_r), jnp.stack(new_c)


def setup_inputs(seed: int = 0) -> dict:
    key = jax.random.key(seed)
    ks = jax.random.split(key, 24)
    f32 = jnp.float32

    def nrm(k, shape, scale):
        return jax.random.normal(k, shape, f32) * scale

    return {
        "x_prompt": nrm(ks[0], (BATCH, SEQ, D_MODEL), 1.0),
        "x_sample": nrm(ks[1], (DEC_BATCH, DEC_SEQ, D_MODEL), 1.0),
        "state_ret": nrm(ks[2], (DEPTH, DEC_BATCH, RET_HEADS, RET_DK, RET_DV), 0.5),
        "state_conv": nrm(ks[3], (DEPTH, DEC_BATCH, CONV_W - 1, CONV_CH), 1.0),
        "p_prompt": nrm(ks[4], (DEPTH, BATCH, SEQ, PLE_DIM), 1.0),
        "p_sample": nrm(ks[5], (DEPTH, DEC_BATCH, DEC_SEQ, PLE_DIM), 1.0),
        "norm_g": 1.0 + nrm(ks[6], (DEPTH, N_NORMS, D_MODEL), 0.02),
        "w_ffn1_in": nrm(ks[7], (DEPTH, D_MODEL, 2 * D_FF), D_MODEL ** -0.5),
        "w_ffn1_out": nrm(ks[8], (DEPTH, D_FF, D_MODEL), D_FF ** -0.5),
        "w_in": nrm(ks[9], (DEPTH, D_MODEL, N_IN), D_MODEL ** -0.5),
        "ret_gn": 1.0 + nrm(ks[10], (DEPTH, D_RET_V), 0.02),
        "w_ret_out": nrm(ks[11], (DEPTH, D_RET_V, D_MODEL), D_RET_V ** -0.5),
        "conv_w": nrm(ks[12], (DEPTH, CONV_W, CONV_CH), CONV_W ** -0.5),
        "w_conv_out": nrm(ks[13], (DEPTH, CONV_CH, D_MODEL), CONV_CH ** -0.5),
        "w_o": nrm(ks[14], (DEPTH, D_MODEL, D_MODEL), D_MODEL ** -0.5),
        "w_ffn2_in": nrm(ks[15], (DEPTH, D_MODEL, 2 * D_FF), D_MODEL ** -0.5),
        "w_ffn2_out": nrm(ks[16], (DEPTH, D_FF, D_MODEL), D_FF ** -0.5),
        "w_ple_gate": nrm(ks[17], (DEPTH, D_MODEL, D_MODEL), D_MODEL ** -0.5),
        "w_ple": nrm(ks[18], (DEPTH, PLE_DIM, D_MODEL), PLE_DIM ** -0.5),
    }


def reference(x_prompt, x_sample, state_ret, state_conv, p_prompt, p_sample, norm_g, w_ffn1_in,
              w_ffn1_out, w_in, ret_gn, w_ret_out, conv_w, w_conv_out, w_o, w_ffn2_in, w_ffn2_out,
              w_ple_gate, w_ple):
    weights = (norm_g, w_ffn1_in, w_ffn1_out, w_in, ret_gn, w_ret_out, conv_w, w_conv_out, w_o,
               w_ffn2_in, w_ffn2_out, w_ple_gate, w_ple)
    s_ret0 = jnp.zeros((DEPTH, BATCH, RET_HEADS, RET_DK, RET_DV), x_prompt.dtype)
    s_conv0 = jnp.zeros((DEPTH, BATCH, CONV_W - 1, CONV_CH), x_prompt.dtype)
    pos_p = jnp.arange(SEQ, dtype=jnp.float32)
    y_prompt, ret_p, conv_p = trunk(x_prompt, p_prompt, s_ret0, s_conv0, pos_p, *weights)
    pos_s = PAST_LEN + jnp.arange(DEC_SEQ, dtype=jnp.float32)
    y_sample, ret_s, conv_s = trunk(x_sample, p_sample, state_ret.astype(x_sample.dtype),
                                    state_conv, pos_s, *weights)
    return (y_prompt, y_sample, ret_p, conv_p, ret_s, conv_s)
```

```python
import numpy as np
import concourse.bass as bass
import concourse.mybir as mybir
from concourse.bass_utils import run_bass_kernel_spmd

F32 = mybir.dt.float32
BF16 = mybir.dt.bfloat16
ALU = mybir.AluOpType
AF = mybir.ActivationFunctionType

D = 1024
DEPTH = 4
SEQ = 2048
NCORES = 8
DEC_PER_CORE = 16
DEC_SEQ = 4
PAST_LEN = 16384
H = 4
DFF = 2816
NJ = DFF // 128
PLE = 256
EPS = 1e-6
PT = 512
ST = 16
T = PT + ST
HT = T // 2
SEGS = ((0, HT, 0), (HT, T, 512))
WIN_EXT = 5632 + 1024
OFF = dict(q=0, k=512, v=1024, g=1536, cb=2048, cc=2560, ch=3072, ga=3584, gb=4608, qp=5632, kp=6144)


class Buf:
    __slots__ = ("name", "w", "r")

    def __init__(self, name):
        self.name = name
        self.w = None
        self.r = []


class Sched:
    def __init__(self, nc):
        self.nc = nc
        self.eng = {"pe": nc.tensor, "act": nc.scalar, "dve": nc.vector, "pool": nc.gpsimd, "sp": nc.sync}
        self.sems = {}
        self.cnt = {}
        self.seen = {e: {} for e in self.eng}
        for e in self.eng:
            self.sems[e] = nc.alloc_semaphore("s_" + e)
            self.cnt[e] = 0
        self.nins = 0
        self.nwaits = 0

    def new_sem(self, name):
        self.sems[name] = self.nc.alloc_semaphore(name)
        self.cnt[name] = 0
        return name

    def _wait(self, e, deps):
        best = {}
        for d in deps:
            if d is None:
                continue
            k, v = d
            if best.get(k, 0) < v:
                best[k] = v
        for k, v in best.items():
            if self.seen[e].get(k, 0) >= v:
                continue
            if k == e and (e == "pe" or v > self.cnt[e]):
                continue
            self.eng[e].wait_ge(self.sems[k], v)
            self.seen[e][k] = v
            self.nwaits += 1

    @staticmethod
    def _deps(reads, writes):
        deps = []
        for b in reads:
            deps.append(b.w)
        for b in writes:
            deps.append(b.w)
            deps.extend(b.r)
        return deps

    @staticmethod
    def _record(ev, reads, writes):
        for b in reads:
            b.r.append(ev)
            if len(b.r) > 64:
                best = {}
                for k, v in b.r:
                    if best.get(k, 0) < v:
                        best[k] = v
                b.r = list(best.items())
        for b in writes:
            b.w = ev
            b.r = []

    def op(self, e, fn, reads=(), writes=(), inc=True):
        self._wait(e, self._deps(reads, writes))
        ins = fn(self.eng[e])
        self.nins += 1
        if inc:
            self.cnt[e] += 1
            ins.then_inc(self.sems[e], 1)
            ev = (e, self.cnt[e])
        else:
            ev = (e, self.cnt[e] + 1)
        self._record(ev, reads, writes)
        return ev

    def dma_group(self, q, sem, items):
        deps = []
        for (_, _, rd, wr) in items:
            deps.extend(self._deps(rd, wr))
        self._wait(q, deps)
        for (o, i, _, _) in items:
            self.eng[q].dma_start(out=o, in_=i).then_inc(self.sems[sem], 16)
            self.cnt[sem] += 16
            self.nins += 1
        ev = (sem, self.cnt[sem])
        for (_, _, rd, wr) in items:
            self._record(ev, rd, wr)
        return ev


class WStream:
    def __init__(self, S, nc, name, nslots, shape, plan):
        self.S = S
        self.n = nslots
        self.plan = plan
        self.tiles = [nc.alloc_sbuf_tensor(f"{name}{i}", [128] + list(shape), BF16) for i in range(nslots)]
        self.bufs = [Buf(f"{name}{i}") for i in range(nslots)]
        self.sems = [S.new_sem(f"d_{name}{i}") for i in range(nslots)]
        self.issued = 0
        self.used = 0
        self.released = 0

    def prefetch(self):
        while self.issued < len(self.plan) and self.issued < self.released + self.n:
            i = self.issued
            key, src, nk, ncols = self.plan[i]
            s = i % self.n
            self.S.dma_group("pool", self.sems[s], [(self.tiles[s][:, 0:nk, 0:ncols], src, [], [self.bufs[s]])])
            self.issued += 1

    def get(self, key):
        assert self.plan[self.used][0] == key, (self.plan[self.used][0], key)
        self.prefetch()
        assert self.used < self.issued
        s = self.used % self.n
        self.used += 1
        return self.tiles[s], self.bufs[s]

    def release(self, k=1):
        self.released += k
        assert self.released <= self.used
        self.prefetch()


def _gammas():
    return 1.0 - np.exp2(-5.0 - np.arange(H, dtype=np.float64))


def make_tables(nt):
    g = _gammas()
    inv_freq = (np.float32(10000.0) ** (-(np.arange(0, 128, 2, dtype=np.float32) / np.float32(128)))).astype(np.float32)
    rotq = np.zeros((nt, 128, H, 2, T), np.float32)
    rotk = np.zeros((nt, 128, 2, T), np.float32)
    f = np.arange(128)
    sign = np.where(f < 64, -1.0, 1.0)
    for t in range(nt):
        pos = np.zeros(T, np.float32)
        loc = np.zeros(T, np.float64)
        pos[:PT] = t * PT + np.arange(PT)
        loc[:PT] = np.arange(PT) % 128
        for s in range(4):
            for j in range(4):
                pos[PT + 4 * s + j] = PAST_LEN + j
                loc[PT + 4 * s + j] = j
        ang = (pos[None, :] * inv_freq[f % 64][:, None]).astype(np.float32)
        cos = np.cos(ang).astype(np.float32).astype(np.float64)
        sin = np.sin(ang).astype(np.float32).astype(np.float64) * sign[:, None]
        for h in range(H):
            dec = g[h] ** (loc + 1.0)
            rotq[t, :, h, 0, :] = cos * dec[None, :]
            rotq[t, :, h, 1, :] = sin * dec[None, :]
        rotk[t, :, 0, :] = cos * (128.0 ** -0.5)
        rotk[t, :, 1, :] = sin * (128.0 ** -0.5)
    e = np.arange(128)
    maskT = np.zeros((128, H, 128), np.float32)
    kdec = np.zeros((128, H), np.float32)
    for h in range(H):
        m = (e[None, :] >= e[:, None]) * (g[h] ** (-(e[:, None] + 1.0)))
        maskT[:, h, :] = m
        kdec[:, h] = g[h] ** (127.0 - e)
    smask = np.zeros((16, H, 16), np.float32)
    smd = np.zeros((16, 16), np.float32)
    for ee in range(16):
        se, je = divmod(ee, 4)
        for h in range(H):
            smd[ee, se * 4 + h] = g[h] ** (3.0 - je)
            for c in range(16):
                sc_, jc = divmod(c, 4)
                if sc_ == se and jc >= je:
                    smask[ee, h, c] = g[h] ** (-(je + 1.0))
    ident = np.eye(128, dtype=np.float32)
    return dict(rotq=rotq, rotk=rotk, maskT=maskT, kdec=kdec, smask=smask, smd=smd, ident=ident)


def build(NT=4, NL=4, dbg=None):
    gam = _gammas()
    GC = [float(gam[h] ** 128.0) for h in range(H)]
    G4 = [float(gam[h] ** 4.0) for h in range(H)]
    nc = bass.Bass("TRN2", target_bir_lowering=False)

    def din(name, shape):
        return nc.dram_tensor(name, list(shape), F32, kind="ExternalInput")

    def dout(name, shape):
        return nc.dram_tensor(name, list(shape), F32, kind="ExternalOutput")

    xp_d = din("xp", [128, 8, NT * PT])
    xs_d = din("xs", [128, 8, NT * ST])
    pp_d = din("pp", [NL, 128, 2, NT * PT])
    psm_d = din("psm", [NL, 128, 2, NT * ST])
    sr_d = din("sr", [NL, 128, NT * 16, 128])
    sc_d = din("sc", [NL, 128, 4, NT * 4, 2])
    w1i_d = din("w1i", [NL, D, 2 * DFF])
    w1o_d = din("w1o", [NL, DFF, D])
    win_d = din("win", [NL, D, WIN_EXT])
    wro_d = din("wro", [NL, 512, D])
    wco_d = din("wco", [NL, 512, D])
    wo_d = din("wo", [NL, D, D])
    w2i_d = din("w2i", [NL, D, 2 * DFF])
    w2o_d = din("w2o", [NL, DFF, D])
    wpg_d = din("wpg", [NL, D, D])
    wpl_d = din("wpl", [NL, PLE, D])
    ng_d = din("ng", [128, NL * 8 * 8])
    gn_d = din("gn", [128, NL * 4])
    cw_d = din("cw", [128, NL * 3 * 4])
    rotq_d = din("rotq", [NT, 128, H * 2 * T])
    rotk_d = din("rotk", [NT, 128, 2 * T])
    maskT_d = din("maskT", [128, H * 128])
    kdec_d = din("kdec", [128, H])
    smask_d = din("smask", [16, H * 16])
    smd_d = din("smd", [16, 16])
    ident_d = din("ident", [128, 128])

    yp_d = dout("yp", [128, 8, NT * PT])
    ys_d = dout("ys", [128, 8, NT * ST])
    retp_d = dout("retp", [NL, 128, H, 128])
    convp_d = dout("convp", [NL, 128, 4, 2])
    rets_d = dout("rets", [NL, 128, NT * 16, 128])
    convs_d = dout("convs", [NL, 128, 4, NT * 4, 2])

    S = Sched(nc)

    def sb(name, shape, dt=F32):
        return nc.alloc_sbuf_tensor("sb_" + name, list(shape), dt)

    h = sb("h", [128, 8, T]); Bhk = [Buf(f"h{k}") for k in range(8)]
    xn = sb("xn", [128, 8, T], BF16); Bxn = [Buf(f"xn{k}") for k in range(8)]
    sq = sb("sq", [128, 8, T], BF16); Bsqk = [Buf(f"sq{k}") for k in range(8)]
    y = sb("y", [128, 8, T]); By = [Buf(f"y{k}") for k in range(8)]
    arena = sb("arena", [128, NJ, T], BF16); Bar = [Buf(f"ar{j}") for j in range(NJ)]
    sgt = [sb(f"sgt{i}", [128, T]) for i in range(2)]; Bsgt = [Buf(f"sgt{i}") for i in range(2)]
    r1 = sb("r1", [128, T]); Br1 = Buf("r1")
    rstd = sb("rstd", [128, T]); Brstd = Buf("rstd")
    r1b = sb("r1b", [128, T]); Br1b = Buf("r1b")
    rstdb = sb("rstdb", [128, T]); Brstdb = Buf("rstdb")
    ones_m = sb("ones_m", [128, 128], BF16)
    ones_g = sb("ones_g", [128, 128], BF16)
    epsc = sb("epsc", [128, 1])
    dumm = sb("dumm", [128, 2]); Bdum = Buf("dumm")
    Bconst = Buf("const")
    qd = sb("qd", [128, H, T], BF16); Bqd = [Buf(f"qd{i}") for i in range(H)]
    kT = sb("kT", [128, H, T], BF16); BkT = [Buf(f"kT{i}") for i in range(H)]
    t1 = sb("t1", [128, T]); Bt1 = Buf("t1")
    t2 = sb("t2", [128, T]); Bt2 = Buf("t2")
    v_tok = sb("v_tok", [128, 4, 512], BF16); Bvt = [Buf(f"vt{i}") for i in range(4)]
    v_toks = sb("v_toks", [16, 512], BF16); Bvts = Buf("vts")
    kd_tok = sb("kd_tok", [128, 4, 512], BF16); Bkd = [Buf(f"kd{i}") for i in range(4)]
    kb = sb("kb", [16, 2, 4, 128], BF16); Bkb = [Buf(f"kb{i}") for i in range(2)]
    sg = sb("sg", [128, H, T], BF16); Bsg = [Buf(f"sg{i}") for i in range(H)]
    sTm = [sb(f"sTm{i}", [128, 128], BF16) for i in range(4)]; BsTm = [Buf(f"sTm{i}") for i in range(4)]
    sTms = sb("sTms", [16, H, 16], BF16); BsTms = [Buf(f"sTms{i}") for i in range(H)]
    Sst = sb("Sst", [128, NL, H, 128]); BSst = [[Buf(f"Sst{l}_{i}") for i in range(H)] for l in range(NL)]
    Sbf = sb("Sbf", [128, H, 128], BF16); BSbf = [Buf(f"Sbf{i}") for i in range(H)]
    S0f = sb("S0f", [128, 16, 128]); BS0f = Buf("S0f")
    S0b = sb("S0b", [128, 16, 128], BF16); BS0b = Buf("S0b")
    cc_sb = [sb(f"cc_sb{i}", [128, T]) for i in range(2)]; Bcc = [Buf(f"cc{i}") for i in range(2)]
    full_p = sb("full_p", [128, 2, PT + 2]); Bfp = [Buf(f"fp{i}") for i in range(2)]
    full_s = sb("full_s", [128, 4, 4, 6]); Bfs = Buf("fs")
    carry = sb("carry", [128, NL, 4, 2]); Bcar = [[Buf(f"car{l}_{j}") for j in range(4)] for l in range(NL)]
    c0 = sb("c0", [128, T]); Bc0 = Buf("c0")
    c1 = sb("c1", [128, T]); Bc1 = Buf("c1")
    mean_sb, Bmean = c0, Bc0
    m2, Bm2 = c1, Bc1
    ma, Bma = t1, Bt1
    mb, Bmb = t2, Bt2
    pT = sb("pT", [128, 2, T], BF16); BpT = Buf("pT")
    rotq = sb("rotq", [128, H, 2, T], BF16); rotk = sb("rotk", [128, 2, T]); Brot = Buf("rot")
    maskT = sb("maskT", [128, H, 128]); kdec = sb("kdec", [128, H])
    smask = sb("smask", [16, H, 16]); smd = sb("smd", [16, 16])
    ident = sb("ident", [128, 128], BF16)
    ng = sb("ng", [128, NL, 8, 8]); ngc = sb("ngc", [128, NL, 8, 8]); gnp = sb("gnp", [128, NL, 4]); cwp = sb("cwp", [128, NL, 3, 4])

    ps_all = nc.alloc_psum_tensor("ps_all", [128, 4096], F32)
    Bbank = [Buf(f"bank{i}") for i in range(8)]
    ctr = {"big": 0, "small": 0}

    def PV(ps):
        return ps.rearrange("p (s c) -> p s c", s=2)[:, :, 0:HT]

    def SV(ap):
        return ap.rearrange("p (s c) -> p s c", s=2)

    def big():
        s = ctr["big"] % 3
        ctr["big"] += 1
        return ps_all[:, s * 1024:(s + 1) * 1024], [Bbank[2 * s], Bbank[2 * s + 1]]

    def small():
        s = ctr["small"] % 6
        ctr["small"] += 1
        return ps_all[:, s * 512:(s + 1) * 512], [Bbank[s]]

    PSTAT = ps_all[:, 3072:4096]
    Bstat = [Bbank[6], Bbank[7]]

    for nm in ("d_const", "d_ident", "d_h", "d_rotq", "d_rotk", "d_p", "d_s0f", "d_s0b", "d_sc", "d_hout", "d_rets", "d_convs", "d_retp", "d_convp"):
        S.new_sem(nm)

    def wblk(wd, l, c0_, K, ncols=256):
        return wd.ap()[l, :, c0_:c0_ + ncols].rearrange("(k p) c -> p k c", p=128)

    planA, planB = [], []
    for t in range(NT):
        for l in range(NL):
            for b in range(11):
                planA.append((("ffi", 1, t, l, b, "g"), wblk(w1i_d, l, b * 256, D), 8, 256))
                planA.append((("ffi", 1, t, l, b, "u"), wblk(w1i_d, l, DFF + b * 256, D), 8, 256))
            for b in range(2):
                planA.append((("win", t, l, "v", b), wblk(win_d, l, OFF["v"] + b * 256, D), 8, 256))
            for nm, nb in (("q", 2), ("k", 2)):
                for b in range(nb):
                    planA.append((("win", t, l, nm, b), wblk(win_d, l, OFF[nm] + b * 256, D), 8, 256))
            for b in range(2):
                planA.append((("win", t, l, "g", b), wblk(win_d, l, OFF["g"] + b * 256, D), 8, 256))
            for b in range(2):
                planA.append((("win", t, l, "cc", b), wblk(win_d, l, OFF["cc"] + b * 256, D), 8, 256))
                planA.append((("win", t, l, "ch", b), wblk(win_d, l, OFF["ch"] + b * 256, D), 8, 256))
            for b in range(2):
                planA.append((("win", t, l, "cb", b), wblk(win_d, l, OFF["cb"] + b * 256, D), 8, 256))
            for nm in ("ga", "gb"):
                for b in range(4):
                    planA.append((("win", t, l, nm, b), wblk(win_d, l, OFF[nm] + b * 256, D), 8, 256))
            for b in range(4):
                planA.append((("wro", t, l, b), wblk(wro_d, l, b * 256, 512), 4, 256))
                planA.append((("wco", t, l, b), wblk(wco_d, l, b * 256, 512), 4, 256))
            for b in range(4):
                planA.append((("wo", t, l, b), wblk(wo_d, l, b * 256, D), 8, 256))
            for b in range(11):
                planA.append((("ffi", 2, t, l, b, "g"), wblk(w2i_d, l, b * 256, D), 8, 256))
                planA.append((("ffi", 2, t, l, b, "u"), wblk(w2i_d, l, DFF + b * 256, D), 8, 256))
            for b in range(4):
                planA.append((("wpg", t, l, b), wblk(wpg_d, l, b * 256, D), 8, 256))
                planA.append((("wpl", t, l, b), wblk(wpl_d, l, b * 256, PLE), 2, 256))
            for m in range(8):
                planB.append((("ffo", 1, t, l, m), wblk(w1o_d, l, m * 128, DFF, 128), NJ, 128))
            for m in range(8):
                planB.append((("ffo", 2, t, l, m), wblk(w2o_d, l, m * 128, DFF, 128), NJ, 128))
    wsA = WStream(S, nc, "wA", 6, [8, 256], planA)
    wsB = WStream(S, nc, "wB", 3, [NJ, 128], planB)

    S.op("pool", lambda e: e.memset(ones_m[:], 1.0 / 1024.0), writes=[Bconst])
    S.op("pool", lambda e: e.memset(ones_g[:], 1.0 / 128.0), writes=[Bconst])
    S.op("pool", lambda e: e.memset(epsc[:], EPS), writes=[Bconst])
    S.op("pool", lambda e: e.memset(dumm[:], 1.0), writes=[Bconst])
    S.op("pool", lambda e: e.memset(Sst[:], 0.0), writes=[b for row in BSst for b in row])
    S.op("pool", lambda e: e.memset(carry[:], 0.0), writes=[b for row in Bcar for b in row])
    S.dma_group("sp", "d_const", [
        (ng[:], ng_d.ap().rearrange("p (l n k) -> p l n k", l=NL, n=8), [], [Bconst]),
        (gnp[:], gn_d.ap().rearrange("p (l h) -> p l h", l=NL), [], [Bconst]),
        (cwp[:], cw_d.ap().rearrange("p (l w j) -> p l w j", l=NL, w=3), [], [Bconst]),
        (maskT[:], maskT_d.ap().rearrange("p (h c) -> p h c", h=H), [], [Bconst]),
        (kdec[:], kdec_d.ap(), [], [Bconst]),
        (smask[:], smask_d.ap().rearrange("p (h c) -> p h c", h=H), [], [Bconst]),
        (smd[:], smd_d.ap(), [], [Bconst]),
    ])
    S.dma_group("pool", "d_ident", [(ident[:], ident_d.ap(), [], [Bconst])])
    S.op("dve", lambda e: e.tensor_copy(ngc[:], ng[:]), reads=[Bconst], writes=[Bconst])
    for n_ in (1, 3, 5):
        S.op("dve", lambda e: e.tensor_scalar(ngc[:, :, n_, :], ng[:, :, n_, :], 0.5, None, ALU.mult), reads=[Bconst], writes=[Bconst])

    def mm(out, lhsT, rhs, start, stop, reads, writes, inc):
        S.op("pe", lambda e: e.matmul(out, lhsT, rhs, start=start, stop=stop), reads=reads, writes=writes, inc=inc)

    def proj(ps, pbufs, wfn, src, nk, wreads, kbufs):
        for k in range(nk):
            for si, (a, b, o) in enumerate(SEGS):
                mm(ps[:, o:o + HT], wfn(k), src[:, k, a:b], k == 0, k == nk - 1, wreads + [kbufs[k]], pbufs,
                   inc=(si == 1 and k == nk - 1))

    def stats(src, k0, nk, ones):
        ps, pb = big()
        for k in range(nk):
            for si, (a, b, o) in enumerate(SEGS):
                mm(ps[:, o:o + HT], ones[:], src[:, k0 + k, a:b], k == 0, k == nk - 1, [Bconst, Bsqk[k0 + k]], pb,
                   inc=(si == 1 and k == nk - 1))
        return ps, pb

    def rsqrt_into(src, bsrc, dst, bdst):
        S.op("act", lambda e: e.activation(src[:], src[:], AF.Ln), reads=[bsrc], writes=[bsrc])
        S.op("act", lambda e: e.activation(dst[:], src[:], AF.Exp, scale=-0.5), reads=[bsrc], writes=[bdst])

    def preload_ln():
        S.op("act", lambda e: e.activation(dumm[:, 1:2], dumm[:, 0:1], AF.Ln), reads=[Bconst], writes=[Bdum])

    def stat_mm(k):
        for si, (a, b, o) in enumerate(SEGS):
            mm(PSTAT[:, o:o + HT], ones_m[:], sq[:, k, a:b], k == 0, k == 7, [Bconst, Bsqk[k]], Bstat, inc=(si == 1))

    def rstd_from_stat():
        S.op("act", lambda e: e.activation(SV(r1[:]), PV(PSTAT), AF.Ln, bias=epsc[:, 0:1]), reads=Bstat + [Bconst], writes=[Br1])
        S.op("act", lambda e: e.activation(rstd[:], r1[:], AF.Exp, scale=-0.5), reads=[Br1], writes=[Brstd])

    def prenorm_stats():
        for k in range(8):
            S.op("act", lambda e: e.activation(sq[:, k, :], h[:, k, :], AF.Square), reads=[Bhk[k]], writes=[Bsqk[k]])
            if k >= 1:
                stat_mm(k - 1)
        stat_mm(7)

    def prenorm_tail(l, n):
        rstd_from_stat()
        for k in range(8):
            S.op("dve", lambda e: e.scalar_tensor_tensor(xn[:, k, :], h[:, k, :], ng[:, l, n, k:k + 1], rstd[:],
                                                          ALU.mult, ALU.mult),
                 reads=[Bhk[k], Bconst, Brstd], writes=[Bxn[k]])

    def evac_y(m, ps, pb, l, n, sq_scale):
        S.op("act", lambda e: e.activation(SV(sq[:, m, :]), PV(ps), AF.Square, scale=sq_scale), reads=pb, writes=[Bsqk[m]])
        S.op("act", lambda e: e.activation(SV(y[:, m, :]), PV(ps), AF.Copy, scale=ngc[:, l, n, m:m + 1]),
             reads=pb + [Bconst], writes=[By[m]])
        if m >= 1:
            stat_mm(m - 1)

    def post_update(l, n, coef, g_applied):
        stat_mm(7)
        rstd_from_stat()
        for k in range(8):
            if g_applied:
                S.op("dve", lambda e: e.tensor_tensor(y[:, k, :], y[:, k, :], rstd[:], ALU.mult),
                     reads=[By[k], Brstd], writes=[By[k]])
                S.op("dve", lambda e: e.tensor_tensor(h[:, k, :], h[:, k, :], y[:, k, :], ALU.add),
                     reads=[By[k], Bhk[k]], writes=[Bhk[k]])
            else:
                S.op("dve", lambda e: e.scalar_tensor_tensor(y[:, k, :], y[:, k, :], ng[:, l, n, k:k + 1], rstd[:],
                                                              ALU.mult, ALU.mult),
                     reads=[By[k], Bconst, Brstd], writes=[By[k]])
                S.op("dve", lambda e: e.scalar_tensor_tensor(h[:, k, :], y[:, k, :], float(coef), h[:, k, :], ALU.mult, ALU.add),
                     reads=[By[k], Bhk[k]], writes=[Bhk[k]])
            S.op("act", lambda e: e.activation(sq[:, k, :], h[:, k, :], AF.Square), reads=[Bhk[k]], writes=[Bsqk[k]])
            if k >= 1:
                stat_mm(k - 1)
        stat_mm(7)

    def ffn(which, t, l, npre, npost):
        prenorm_tail(l, npre)
        for b in range(11):
            gw, gwb = wsA.get(("ffi", which, t, l, b, "g"))
            uw, uwb = wsA.get(("ffi", which, t, l, b, "u"))
            for jj in range(2):
                j = 2 * b + jj
                pg, pgb = big()
                proj(pg, pgb, lambda k: gw[:, k, jj * 128:(jj + 1) * 128], xn, 8, [gwb], Bxn)
                pu, pub = big()
                proj(pu, pub, lambda k: uw[:, k, jj * 128:(jj + 1) * 128], xn, 8, [uwb], Bxn)
                st = sgt[j % 2]
                S.op("act", lambda e: e.activation(SV(st[:]), PV(pg), AF.Silu), reads=pgb, writes=[Bsgt[j % 2]])
                S.op("dve", lambda e: e.tensor_tensor(SV(arena[:, j, :]), SV(st[:]), PV(pu), ALU.mult),
                     reads=[Bsgt[j % 2]] + pub, writes=[Bar[j]])
            wsA.release(2)
        preload_ln()
        for m in range(8):
            ow, owb = wsB.get(("ffo", which, t, l, m))
            po, pob = big()
            proj(po, pob, lambda k: ow[:, k, :], arena, NJ, [owb], Bar)
            evac_y(m, po, pob, l, npost, 1.0)
            wsB.release(1)
        if dbg == "ffo":
            S.op("dve", lambda e: e.tensor_copy(h[:], y[:]), reads=By, writes=Bhk)
            return
        if dbg == "ffi":
            S.op("dve", lambda e: e.tensor_copy(h[:], arena[:, 0:8, :]), reads=Bar, writes=Bhk)
            return
        post_update(l, npost, 0.5, True)

    def mixer(t, l):
        prenorm_tail(l, 2)
        w0, w0b = wsA.get(("win", t, l, "v", 0))
        w1_, w1b = wsA.get(("win", t, l, "v", 1))
        for n in range(4):
            pv, pvb = small()
            for wi, (w, wb_) in enumerate(((w0, w0b), (w1_, w1b))):
                for k in range(8):
                    mm(pv[:, wi * 256:(wi + 1) * 256], xn[:, k, n * 128:(n + 1) * 128], w[:, k, :], k == 0, k == 7,
                       [wb_, Bxn[k]], pvb, inc=(wi == 1 and k == 7))
            S.op("act", lambda e: e.activation(v_tok[:, n, :], pv[:, 0:512], AF.Copy), reads=pvb, writes=[Bvt[n]])
        pv, pvb = small()
        for wi, (w, wb_) in enumerate(((w0, w0b), (w1_, w1b))):
            for k in range(8):
                mm(pv[0:16, wi * 256:(wi + 1) * 256], xn[:, k, PT:T], w[:, k, :], k == 0, k == 7,
                   [wb_, Bxn[k]], pvb, inc=(wi == 1 and k == 7))
        S.op("act", lambda e: e.activation(v_toks[:], pv[0:16, 0:512], AF.Copy), reads=pvb, writes=[Bvts])
        wsA.release(2)
        def k_transposes(hh):
            for n in range(4):
                tp, tpb = small()
                tpv = tp.bitcast(BF16)
                S.op("pe", lambda e: e.transpose(tpv[:, 0:128], kT[:, hh, n * 128:(n + 1) * 128], ident[:]),
                     reads=[BkT[hh], Bconst], writes=tpb)
                S.op("act", lambda e: e.activation(kd_tok[:, n, hh * 128:(hh + 1) * 128], tpv[:, 0:128], AF.Copy,
                                                   scale=kdec[:, hh:hh + 1]),
                     reads=tpb + [Bconst], writes=[Bkd[n]])

        def PVp(ps, p0, p1):
            return ps[p0:p1].rearrange("p (s c) -> p s c", s=2)[:, :, 0:HT]

        for nm in ("q", "k"):
            for b in range(2):
                w, wb_ = wsA.get(("win", t, l, nm, b))
                for jj in range(2):
                    hh = 2 * b + jj
                    pa, pab = big()
                    proj(pa, pab, lambda k: w[:, k, jj * 128:(jj + 1) * 128], xn, 8, [wb_], Bxn)
                    if nm == "q":
                        cs, sn, dst, dbuf = rotq[:, hh, 0, :], rotq[:, hh, 1, :], qd, Bqd[hh]
                    else:
                        cs, sn, dst, dbuf = rotk[:, 0, :], rotk[:, 1, :], kT, BkT[hh]
                    S.op("dve", lambda e: e.tensor_tensor(SV(t1[:]), PV(pa), SV(cs), ALU.mult), reads=pab + [Brot], writes=[Bt1])
                    S.op("dve", lambda e: e.tensor_tensor(SV(t2[0:64, :]), PVp(pa, 64, 128), SV(sn[0:64]), ALU.mult),
                         reads=pab + [Brot], writes=[Bt2])
                    S.op("dve", lambda e: e.tensor_tensor(SV(t2[64:128, :]), PVp(pa, 0, 64), SV(sn[64:128]), ALU.mult),
                         reads=pab + [Brot, Bt2], writes=[Bt2])
                    S.op("dve", lambda e: e.tensor_tensor(dst[:, hh, :], t1[:], t2[:], ALU.add), reads=[Bt1, Bt2], writes=[dbuf])
                    if nm == "k":
                        if hh >= 1:
                            k_transposes(hh - 1)
                wsA.release(1)
        for b in range(2):
            w, wb_ = wsA.get(("win", t, l, "g", b))
            for jj in range(2):
                hh = 2 * b + jj
                pg, pgb = big()
                proj(pg, pgb, lambda k: w[:, k, jj * 128:(jj + 1) * 128], xn, 8, [wb_], Bxn)
                S.op("act", lambda e: e.activation(SV(sg[:, hh, :]), PV(pg), AF.Silu), reads=pgb, writes=[Bsg[hh]])
            wsA.release(1)
        k_transposes(H - 1)
        for b in range(2):
            cwb_, cwbb = wsA.get(("win", t, l, "cc", b))
            hwb_, hwbb = wsA.get(("win", t, l, "ch", b))
            for jj in range(2):
                j = 2 * b + jj
                pc, pcb = big()
                proj(pc, pcb, lambda k: cwb_[:, k, jj * 128:(jj + 1) * 128], xn, 8, [cwbb], Bxn)
                ccs = cc_sb[jj]
                S.op("act", lambda e: e.activation(SV(ccs[:]), PV(pc), AF.Copy), reads=pcb, writes=[Bcc[jj]])
                ph, phb = big()
                proj(ph, phb, lambda k: hwb_[:, k, jj * 128:(jj + 1) * 128], xn, 8, [hwbb], Bxn)
                S.op("act", lambda e: e.activation(SV(t1[:]), PV(ph), AF.Copy), reads=phb, writes=[Bt1])
                S.op("dve", lambda e: e.tensor_copy(full_p[:, j % 2, 0:2], carry[:, l, j, :]), reads=[Bcar[l][j]], writes=[Bfp[j % 2]])
                S.op("dve", lambda e: e.tensor_tensor(full_p[:, j % 2, 2:PT + 2], ccs[:, 0:PT], t1[:, 0:PT], ALU.mult),
                     reads=[Bcc[jj], Bt1], writes=[Bfp[j % 2]])
                S.op("dve", lambda e: e.tensor_tensor(full_s[:, j, :, 2:6],
                                                      ccs[:, PT:T].rearrange("p (s w) -> p s w", w=4),
                                                      t1[:, PT:T].rearrange("p (s w) -> p s w", w=4), ALU.mult),
                     reads=[Bcc[jj], Bt1], writes=[Bfs])
                S.op("dve", lambda e: e.tensor_copy(carry[:, l, j, :], full_p[:, j % 2, PT:PT + 2]), reads=[Bfp[j % 2]], writes=[Bcar[l][j]])
                cvo = y[:, 4 + j, :]
                S.op("dve", lambda e: e.tensor_scalar(c0[:, 0:PT], full_p[:, j % 2, 0:PT], cwp[:, l, 0, j:j + 1], None, ALU.mult),
                     reads=[Bfp[j % 2], Bconst], writes=[Bc0])
                S.op("dve", lambda e: e.scalar_tensor_tensor(c1[:, 0:PT], full_p[:, j % 2, 1:PT + 1], cwp[:, l, 1, j:j + 1], c0[:, 0:PT],
                                                              ALU.mult, ALU.add),
                     reads=[Bfp[j % 2], Bconst, Bc0], writes=[Bc1])
                S.op("dve", lambda e: e.scalar_tensor_tensor(cvo[:, 0:PT], full_p[:, j % 2, 2:PT + 2], cwp[:, l, 2, j:j + 1], c1[:, 0:PT],
                                                              ALU.mult, ALU.add),
                     reads=[Bfp[j % 2], Bconst, Bc1], writes=[By[4 + j]])
                v3 = lambda ap: ap.rearrange("p (s w) -> p s w", w=4)
                S.op("dve", lambda e: e.tensor_scalar(v3(c0[:, PT:T]), full_s[:, j, :, 0:4], cwp[:, l, 0, j:j + 1], None, ALU.mult),
                     reads=[Bfs, Bconst], writes=[Bc0])
                S.op("dve", lambda e: e.scalar_tensor_tensor(v3(c1[:, PT:T]), full_s[:, j, :, 1:5], cwp[:, l, 1, j:j + 1], v3(c0[:, PT:T]),
                                                              ALU.mult, ALU.add),
                     reads=[Bfs, Bconst, Bc0], writes=[Bc1])
                S.op("dve", lambda e: e.scalar_tensor_tensor(v3(cvo[:, PT:T]), full_s[:, j, :, 2:6], cwp[:, l, 2, j:j + 1], v3(c1[:, PT:T]),
                                                              ALU.mult, ALU.add),
                     reads=[Bfs, Bconst, Bc1], writes=[By[4 + j]])
            wsA.release(2)
        S.dma_group("sp", "d_convs", [(convs_d.ap()[l, :, :, t * 4:(t + 1) * 4, :], full_s[:, :, :, 4:6], [Bfs], [])])
        if t == NT - 1:
            S.dma_group("sp", "d_convp", [(convp_d.ap()[l], carry[:, l, :, :], Bcar[l], [])])
        for b in range(2):
            w, wb_ = wsA.get(("win", t, l, "cb", b))
            for jj in range(2):
                j = 2 * b + jj
                pcb_, pcbb = big()
                proj(pcb_, pcbb, lambda k: w[:, k, jj * 128:(jj + 1) * 128], xn, 8, [wb_], Bxn)
                S.op("dve", lambda e: e.tensor_tensor(SV(arena[:, 16 + j, :]), SV(y[:, 4 + j, :]), PV(pcb_), ALU.mult),
                     reads=[By[4 + j]] + pcbb, writes=[Bar[16 + j]])
            wsA.release(1)
        for hh in range(H):
            S.op("act", lambda e: e.activation(Sbf[:, hh, :], Sst[:, l, hh, :], AF.Copy), reads=[BSst[l][hh]], writes=[BSbf[hh]])
        for n in range(4):
            cols = slice(n * 128, (n + 1) * 128)
            hsl = [slice(hh * 128, (hh + 1) * 128) for hh in range(H)]
            pss, pus, pos = [], [], []
            for hh in range(H):
                ps_, psb = small()
                mm(ps_[:, 0:128], kT[:, hh, cols], qd[:, hh, cols], True, True, [BkT[hh], Bqd[hh]], psb, True)
                pss.append((ps_, psb))
            for hh in range(H):
                ps_, psb = pss[hh]
                S.op("dve", lambda e: e.tensor_tensor(sTm[hh][:], ps_[:, 0:128], maskT[:, hh, :], ALU.mult),
                     reads=psb + [Bconst], writes=[BsTm[hh]])
            for hh in range(H):
                po_, pob = small()
                mm(po_[:, 0:128], v_tok[:, n, hsl[hh]], sTm[hh][:], True, False, [Bvt[n], BsTm[hh]], pob, False)
                mm(po_[:, 0:128], Sbf[:, hh, :], qd[:, hh, cols], False, True, [BSbf[hh], Bqd[hh]], pob, True)
                pos.append((po_, pob))
            for hh in range(H):
                po_, pob = pos[hh]
                S.op("act", lambda e: e.activation(y[:, hh, cols], po_[:, 0:128], AF.Copy), reads=pob, writes=[By[hh]])
            for hh in range(H):
                pu_, pub = small()
                mm(pu_[:, 0:128], kd_tok[:, n, hsl[hh]], v_tok[:, n, hsl[hh]], True, True, [Bkd[n], Bvt[n]], pub, True)
                pus.append((pu_, pub))
            for hh in range(H):
                pu_, pub = pus[hh]
                S.op("dve", lambda e: e.scalar_tensor_tensor(Sst[:, l, hh, :], Sst[:, l, hh, :], GC[hh], pu_[:, 0:128],
                                                              ALU.mult, ALU.add),
                     reads=pub + [BSst[l][hh]], writes=[BSst[l][hh]])
                if n < 3:
                    S.op("act", lambda e: e.activation(Sbf[:, hh, :], Sst[:, l, hh, :], AF.Copy),
                         reads=[BSst[l][hh]], writes=[BSbf[hh]])
        if t == NT - 1:
            S.dma_group("sp", "d_retp", [(retp_d.ap()[l], Sst[:, l, :, :], BSst[l], [])])
        for hh in range(H):
            hs = slice(hh * 128, (hh + 1) * 128)
            tp, tpb = small()
            tpv = tp.bitcast(BF16)
            S.op("pe", lambda e: e.transpose(tpv[0:16, 0:128], kT[:, hh, PT:T], ident[:]),
                 reads=[BkT[hh], Bconst], writes=tpb)
            for s in range(4):
                S.op("act", lambda e: e.activation(kb[:, hh % 2, s, :], tpv[0:16, 0:128], AF.Copy,
                                                   scale=smd[:, s * 4 + hh:s * 4 + hh + 1]),
                     reads=tpb + [Bconst], writes=[Bkb[hh % 2]])
            ps_, psb = small()
            mm(ps_[0:16, 0:16], kT[:, hh, PT:T], qd[:, hh, PT:T], True, True, [BkT[hh], Bqd[hh]], psb, True)
            S.op("dve", lambda e: e.tensor_tensor(sTms[:, hh, :], ps_[0:16, 0:16], smask[:, hh, :], ALU.mult),
                 reads=psb + [Bconst], writes=[BsTms[hh]])
            po_, pob = small()
            mm(po_[:, 0:16], v_toks[:, hs], sTms[:, hh, :], True, False, [Bvts, BsTms[hh]], pob, False)
            for s in range(4):
                mm(po_[:, 4 * s:4 * s + 4], S0b[:, s * 4 + hh, :], qd[:, hh, PT + 4 * s:PT + 4 * s + 4], False, s == 3,
                   [BS0b, Bqd[hh]], pob, s == 3)
            S.op("act", lambda e: e.activation(y[:, hh, PT:T], po_[:, 0:16], AF.Copy), reads=pob, writes=[By[hh]])
            for s in range(4):
                pu_, pub = small()
                mm(pu_[:, 0:128], kb[:, hh % 2, s, :], v_toks[:, hs], True, True, [Bkb[hh % 2], Bvts], pub, True)
                S.op("dve", lambda e: e.scalar_tensor_tensor(S0f[:, s * 4 + hh, :], S0f[:, s * 4 + hh, :], G4[hh], pu_[:, 0:128],
                                                              ALU.mult, ALU.add),
                     reads=pub + [BS0f], writes=[BS0f])
        S.dma_group("sp", "d_rets", [(rets_d.ap()[l, :, t * 16:(t + 1) * 16, :], S0f[:], [BS0f], [])])
        gsets = ((c0, Bc0, c1, Bc1, r1, Br1, rstd, Brstd), (t1, Bt1, t2, Bt2, r1b, Br1b, rstdb, Brstdb))

        def gn_a1(hh):
            S.op("dve", lambda e: e.tensor_copy(sq[:, hh, :], y[:, hh, :]), reads=[By[hh]], writes=[Bsqk[hh]])
            S.op("act", lambda e: e.activation(sq[:, 4 + hh, :], y[:, hh, :], AF.Square), reads=[By[hh]], writes=[Bsqk[4 + hh]])

        def gn_a2(hh):
            mean_, bmean_, m2_, bm2_, r_, br_, rs_, brs_ = gsets[hh % 2]
            pm, pmb = stats(sq, hh, 1, ones_g)
            pq, pqb = stats(sq, 4 + hh, 1, ones_g)
            S.op("act", lambda e: e.activation(SV(mean_[:]), PV(pm), AF.Copy), reads=pmb, writes=[bmean_])
            S.op("act", lambda e: e.activation(SV(m2_[:]), PV(pm), AF.Square), reads=pmb, writes=[bm2_])
            S.op("dve", lambda e: e.scalar_tensor_tensor(SV(r_[:]), PV(pq), EPS, SV(m2_[:]), ALU.add, ALU.subtract),
                 reads=pqb + [bm2_], writes=[br_])
            rsqrt_into(r_, br_, rs_, brs_)

        def gn_b(hh):
            mean_, bmean_, m2_, bm2_, r_, br_, rs_, brs_ = gsets[hh % 2]
            S.op("dve", lambda e: e.tensor_tensor(y[:, hh, :], y[:, hh, :], mean_[:], ALU.subtract),
                 reads=[By[hh], bmean_], writes=[By[hh]])
            S.op("dve", lambda e: e.scalar_tensor_tensor(y[:, hh, :], y[:, hh, :], gnp[:, l, hh:hh + 1], rs_[:], ALU.mult, ALU.mult),
                 reads=[By[hh], Bconst, brs_], writes=[By[hh]])
            S.op("dve", lambda e: e.tensor_tensor(sg[:, hh, :], sg[:, hh, :], y[:, hh, :], ALU.mult),
                 reads=[Bsg[hh], By[hh]], writes=[Bsg[hh]])

        def gate_blk(i):
            gi, b = divmod(i, 4)
            w, wb_ = wsA.get(("win", t, l, ("ga", "gb")[gi], b))
            for jj in range(2):
                m = 2 * b + jj
                pg, pgb = big()
                proj(pg, pgb, lambda k: w[:, k, jj * 128:(jj + 1) * 128], xn, 8, [wb_], Bxn)
                S.op("dve", lambda e: e.tensor_copy(SV(arena[:, gi * 8 + m, :]), PV(pg)), reads=pgb, writes=[Bar[gi * 8 + m]])
            wsA.release(1)

        for hh in range(H):
            gn_a1(hh)
        gate_blk(0); gn_a2(0); gate_blk(1); gn_a2(1); gn_b(0); gate_blk(2); gn_a2(2); gn_b(1); gate_blk(3); gn_a2(3); gn_b(2)
        gate_blk(4); gn_b(3)
        for i in range(5, 8):
            gate_blk(i)
        for gi in range(2):
            S.op("act", lambda e: e.activation(arena[:, gi * 8:gi * 8 + 8, :], arena[:, gi * 8:gi * 8 + 8, :], AF.Tanh, scale=0.5),
                 reads=Bar[gi * 8:gi * 8 + 8], writes=Bar[gi * 8:gi * 8 + 8])
        preload_ln()
        for b in range(4):
            rw, rwb = wsA.get(("wro", t, l, b))
            cw_, cwb2 = wsA.get(("wco", t, l, b))
            for jj in range(2):
                m = 2 * b + jj
                pa, pab = big()
                proj(pa, pab, lambda k: rw[:, k, jj * 128:(jj + 1) * 128], sg, 4, [rwb], Bsg)
                pb_, pbb = big()
                proj(pb_, pbb, lambda k: cw_[:, k, jj * 128:(jj + 1) * 128], arena[:, 16:20, :], 4, [cwb2], Bar[16:20])
                S.op("dve", lambda e: e.scalar_tensor_tensor(SV(ma[:]), SV(arena[:, m, :]), 1.0, PV(pa), ALU.add, ALU.mult),
                     reads=[Bar[m]] + pab, writes=[Bma])
                S.op("dve", lambda e: e.scalar_tensor_tensor(SV(mb[:]), SV(arena[:, 8 + m, :]), 1.0, PV(pb_), ALU.add, ALU.mult),
                     reads=[Bar[8 + m]] + pbb, writes=[Bmb])
                S.op("dve", lambda e: e.tensor_tensor(xn[:, m, :], ma[:], mb[:], ALU.add), reads=[Bma, Bmb], writes=[Bxn[m]])
            wsA.release(2)
        for b in range(4):
            ww, wwb = wsA.get(("wo", t, l, b))
            for jj in range(2):
                m = 2 * b + jj
                po, pob = big()
                proj(po, pob, lambda k: ww[:, k, jj * 128:(jj + 1) * 128], xn, 8, [wwb], Bxn)
                evac_y(m, po, pob, l, 3, 0.5)
            wsA.release(1)
        post_update(l, 3, 1.0, True)

    def ple(t, l):
        prenorm_tail(l, 6)
        for b in range(4):
            gw, gwb = wsA.get(("wpg", t, l, b))
            pw, pwb = wsA.get(("wpl", t, l, b))
            for jj in range(2):
                m = 2 * b + jj
                pg, pgb = big()
                proj(pg, pgb, lambda k: gw[:, k, jj * 128:(jj + 1) * 128], xn, 8, [gwb], Bxn)
                pe_, peb = big()
                proj(pe_, peb, lambda k: pw[:, k, jj * 128:(jj + 1) * 128], pT, 2, [pwb], [BpT, BpT])
                S.op("act", lambda e: e.activation(SV(t1[:]), PV(pg), AF.Tanh, scale=0.5), reads=pgb, writes=[Bt1])
                S.op("act", lambda e: e.activation(SV(t2[:]), PV(pe_), AF.Copy, scale=0.5), reads=peb, writes=[Bt2])
                S.op("dve", lambda e: e.scalar_tensor_tensor(y[:, m, :], t1[:], 1.0, t2[:], ALU.add, ALU.mult),
                     reads=[Bt1, Bt2], writes=[By[m]])
                S.op("act", lambda e: e.activation(sq[:, m, :], y[:, m, :], AF.Square), reads=[By[m]], writes=[Bsqk[m]])
                if m >= 1:
                    stat_mm(m - 1)
            wsA.release(2)
        post_update(l, 7, 1.0, False)

    for t in range(NT):
        S.dma_group("sp", "d_h", [
            (h[:, :, 0:PT], xp_d.ap()[:, :, t * PT:(t + 1) * PT], [], Bhk),
            (h[:, :, PT:T], xs_d.ap()[:, :, t * ST:(t + 1) * ST], [], Bhk),
        ])
        prenorm_stats()
        S.dma_group("pool", "d_rotq", [
            (rotq[:], rotq_d.ap()[t].rearrange("p (h c x) -> p h c x", h=H, c=2), [], [Brot]),
        ])
        S.dma_group("sp", "d_rotk", [
            (rotk[:], rotk_d.ap()[t].rearrange("p (c x) -> p c x", c=2), [], [Brot]),
        ])
        for l in range(NL):
            S.dma_group("pool", "d_p", [
                (pT[:, :, 0:PT], pp_d.ap()[l, :, :, t * PT:(t + 1) * PT], [], [BpT]),
                (pT[:, :, PT:T], psm_d.ap()[l, :, :, t * ST:(t + 1) * ST], [], [BpT]),
            ])
            S.dma_group("sp", "d_s0f", [(S0f[:], sr_d.ap()[l, :, t * 16:(t + 1) * 16, :], [], [BS0f])])
            S.dma_group("pool", "d_s0b", [(S0b[:], sr_d.ap()[l, :, t * 16:(t + 1) * 16, :], [], [BS0b])])
            S.dma_group("sp", "d_sc", [(full_s[:, :, :, 0:2], sc_d.ap()[l, :, :, t * 4:(t + 1) * 4, :], [], [Bfs])])
            if dbg == "load":
                break
            if dbg == "pre":
                prenorm_tail(l, 0)
                S.op("dve", lambda e: e.tensor_copy(h[:], xn[:]), reads=Bxn, writes=Bhk)
                break
            ffn(1, t, l, 0, 1)
            if dbg in ("ffn1", "ffo", "ffi"):
                break
            mixer(t, l)
            if dbg == "mixer":
                break
            ffn(2, t, l, 4, 5)
            ple(t, l)
        S.dma_group("sp", "d_hout", [
            (yp_d.ap()[:, :, t * PT:(t + 1) * PT], h[:, :, 0:PT], Bhk, []),
            (ys_d.ap()[:, :, t * ST:(t + 1) * ST], h[:, :, PT:T], Bhk, []),
        ])
    assert dbg or (wsA.used == len(planA) and wsB.used == len(planB) and wsA.released == wsA.used and wsB.released == wsB.used)
    for nm in ("d_hout", "d_rets", "d_convs", "d_retp", "d_convp"):
        if S.cnt[nm] > 0:
            nc.sync.wait_ge(S.sems[nm], S.cnt[nm])
    return nc, S


def _fm(a):
    tok, F = a.shape
    return np.ascontiguousarray(a.T.reshape(F // 128, 128, tok).transpose(1, 0, 2))


def _partner_idx():
    idx = []
    for hh in range(H):
        idx += list(range(hh * 128 + 64, hh * 128 + 128)) + list(range(hh * 128, hh * 128 + 64))
    return np.array(idx)


def prepare_shared(inputs, NL=4, NT=4):
    f32 = lambda a: np.ascontiguousarray(np.asarray(a, dtype=np.float32))
    w_in = f32(inputs["w_in"])[:NL]
    pidx = _partner_idx()
    win_ext = np.ascontiguousarray(np.concatenate([w_in, w_in[:, :, pidx], w_in[:, :, 512 + pidx]], axis=2))
    ng = f32(inputs["norm_g"])[:NL]
    ng_fm = np.ascontiguousarray(ng.reshape(NL, 8, 8, 128).transpose(3, 0, 1, 2).reshape(128, NL * 64))
    gn = f32(inputs["ret_gn"])[:NL]
    gn_fm = np.ascontiguousarray(gn.reshape(NL, 4, 128).transpose(2, 0, 1).reshape(128, NL * 4))
    cw = f32(inputs["conv_w"])[:NL]
    cw_fm = np.ascontiguousarray(cw.reshape(NL, 3, 4, 128).transpose(3, 0, 1, 2).reshape(128, NL * 12))
    tb = make_tables(NT)
    sh = dict(
        w1i=f32(inputs["w_ffn1_in"])[:NL], w1o=f32(inputs["w_ffn1_out"])[:NL], win=win_ext,
        wro=f32(inputs["w_ret_out"])[:NL], wco=f32(inputs["w_conv_out"])[:NL], wo=f32(inputs["w_o"])[:NL],
        w2i=f32(inputs["w_ffn2_in"])[:NL], w2o=f32(inputs["w_ffn2_out"])[:NL],
        wpg=f32(inputs["w_ple_gate"])[:NL], wpl=f32(inputs["w_ple"])[:NL],
        ng=ng_fm, gn=gn_fm, cw=cw_fm,
        rotq=np.ascontiguousarray(tb["rotq"].reshape(NT, 128, H * 2 * T)),
        rotk=np.ascontiguousarray(tb["rotk"].reshape(NT, 128, 2 * T)),
        maskT=np.ascontiguousarray(tb["maskT"].reshape(128, H * 128)), kdec=tb["kdec"],
        smask=np.ascontiguousarray(tb["smask"].reshape(16, H * 16)), smd=tb["smd"], ident=tb["ident"],
    )
    return sh


def prepare_core(inputs, c, NL=4, NT=4):
    f32 = lambda a: np.asarray(a, dtype=np.float32)
    ns = NT * 4
    b0 = c * DEC_PER_CORE
    xp = _fm(f32(inputs["x_prompt"])[c, :NT * PT])
    xs = _fm(f32(inputs["x_sample"])[b0:b0 + ns].reshape(ns * 4, D))
    pp = np.stack([_fm(f32(inputs["p_prompt"])[l, c, :NT * PT]) for l in range(NL)])
    psm = np.stack([_fm(f32(inputs["p_sample"])[l, b0:b0 + ns].reshape(ns * 4, PLE)) for l in range(NL)])
    sr = f32(inputs["state_ret"])[:NL, b0:b0 + ns]
    sr = np.ascontiguousarray(sr.transpose(0, 3, 1, 2, 4).reshape(NL, 128, ns * 4, 128))
    sc = f32(inputs["state_conv"])[:NL, b0:b0 + ns]
    sc = np.ascontiguousarray(sc.reshape(NL, ns, 2, 4, 128).transpose(0, 4, 3, 1, 2))
    return dict(xp=xp, xs=xs, pp=pp, psm=psm, sr=sr, sc=sc)


def assemble(results, NL=4, NT=4):
    nco = len(results)
    ns = NT * 4
    yp = np.zeros((nco, NT * PT, D), np.float32)
    ys = np.zeros((nco * ns, 4, D), np.float32)
    retp = np.zeros((NL, nco, H, 128, 128), np.float32)
    convp = np.zeros((NL, nco, 2, 512), np.float32)
    rets = np.zeros((NL, nco * ns, H, 128, 128), np.float32)
    convs = np.zeros((NL, nco * ns, 2, 512), np.float32)
    for c, r in enumerate(results):
        yp[c] = np.asarray(r["yp"]).transpose(2, 1, 0).reshape(NT * PT, D)
        ys[c * ns:(c + 1) * ns] = np.asarray(r["ys"]).transpose(2, 1, 0).reshape(ns, 4, D)
        retp[:, c] = np.asarray(r["retp"]).transpose(0, 2, 1, 3)
        convp[:, c] = np.asarray(r["convp"]).transpose(0, 3, 2, 1).reshape(NL, 2, 512)
        rets[:, c * ns:(c + 1) * ns] = np.asarray(r["rets"]).reshape(NL, 128, ns, H, 128).transpose(0, 2, 3, 1, 4)
        convs[:, c * ns:(c + 1) * ns] = np.asarray(r["convs"]).transpose(0, 3, 4, 2, 1).reshape(NL, ns, 2, 512)
    return yp, ys, retp, convp, rets, convs


def kernel(**inputs):
    nc, _ = build(4, 4)
    sh = prepare_shared(inputs)
    in_maps = []
    for c in range(NCORES):
        m = dict(sh)
        m.update(prepare_core(inputs, c))
        in_maps.append(m)
    res = run_bass_kernel_spmd(nc, in_maps, core_ids=list(range(NCORES)))
    return assemble(res.results)
```

```python
import numpy as np
import concourse.bass as bass
import concourse.mybir as mybir
from concourse.bass_utils import run_bass_kernel_spmd

F32 = mybir.dt.float32
BF16 = mybir.dt.bfloat16
ALU = mybir.AluOpType
AF = mybir.ActivationFunctionType

D = 1024
DEPTH = 4
SEQ = 2048
NCORES = 8
DEC_PER_CORE = 16
DEC_SEQ = 4
PAST_LEN = 16384
H = 4
DFF = 2816
NJ = DFF // 128
PLE = 256
EPS = 1e-6
PT = 512
ST = 16
T = PT + ST
HT = T // 2
SEGS = ((0, HT, 0), (HT, T, 512))
WIN_EXT = 5632 + 1024
OFF = dict(q=0, k=512, v=1024, g=1536, cb=2048, cc=2560, ch=3072, ga=3584, gb=4608, qp=5632, kp=6144)


class Buf:
    __slots__ = ("name", "w", "r")

    def __init__(self, name):
        self.name = name
        self.w = None
        self.r = []


class Sched:
    def __init__(self, nc):
        self.nc = nc
        self.eng = {"pe": nc.tensor, "act": nc.scalar, "dve": nc.vector, "pool": nc.gpsimd, "sp": nc.sync}
        self.sems = {}
        self.cnt = {}
        self.seen = {e: {} for e in self.eng}
        for e in self.eng:
            self.sems[e] = nc.alloc_semaphore("s_" + e)
            self.cnt[e] = 0
        self.nins = 0
        self.nwaits = 0

    def new_sem(self, name):
        self.sems[name] = self.nc.alloc_semaphore(name)
        self.cnt[name] = 0
        return name

    def _wait(self, e, deps):
        best = {}
        for d in deps:
            if d is None:
                continue
            k, v = d
            if best.get(k, 0) < v:
                best[k] = v
        for k, v in best.items():
            if self.seen[e].get(k, 0) >= v:
                continue
            if k == e and (e == "pe" or v > self.cnt[e]):
                continue
            self.eng[e].wait_ge(self.sems[k], v)
            self.seen[e][k] = v
            self.nwaits += 1

    @staticmethod
    def _deps(reads, writes):
        deps = []
        for b in reads:
            deps.append(b.w)
        for b in writes:
            deps.append(b.w)
            deps.extend(b.r)
        return deps

    @staticmethod
    def _record(ev, reads, writes):
        for b in reads:
            b.r.append(ev)
            if len(b.r) > 64:
                best = {}
                for k, v in b.r:
                    if best.get(k, 0) < v:
                        best[k] = v
                b.r = list(best.items())
        for b in writes:
            b.w = ev
            b.r = []

    def _pending(self, e, deps):
        best = {}
        for d in deps:
            if d is None:
                continue
            k, v = d
            if best.get(k, 0) < v:
                best[k] = v
        out = []
        for k, v in best.items():
            if self.seen[e].get(k, 0) >= v:
                continue
            if k == e and (e == "pe" or v > self.cnt[e]):
                continue
            out.append((k, v))
        return out

    def op(self, e, fn, reads=(), writes=(), inc=True):
        deps = self._deps(reads, writes)
        emb = None
        if e in ("act", "dve"):
            pend = self._pending(e, deps)
            if pend:
                emb = pend[-1]
                deps = [d for d in deps if d is None or d[0] != emb[0]]
        self._wait(e, deps)
        ins = fn(self.eng[e])
        if emb is not None:
            ins._wait_ge(self.sems[emb[0]], emb[1])
            self.seen[e][emb[0]] = emb[1]
            self.nwaits += 1
        self.nins += 1
        if inc:
            self.cnt[e] += 1
            ins.then_inc(self.sems[e], 1)
            ev = (e, self.cnt[e])
        else:
            ev = (e, self.cnt[e] + 1)
        self._record(ev, reads, writes)
        return ev

    def dma_group(self, q, sem, items):
        deps = []
        for (_, _, rd, wr) in items:
            deps.extend(self._deps(rd, wr))
        self._wait(q, deps)
        for (o, i, _, _) in items:
            self.eng[q].dma_start(out=o, in_=i).then_inc(self.sems[sem], 16)
            self.cnt[sem] += 16
            self.nins += 1
        ev = (sem, self.cnt[sem])
        for (_, _, rd, wr) in items:
            self._record(ev, rd, wr)
        return ev


class WStream:
    def __init__(self, S, nc, name, nslots, shape, plan):
        self.S = S
        self.n = nslots
        self.plan = plan
        self.tiles = [nc.alloc_sbuf_tensor(f"{name}{i}", [128] + list(shape), BF16) for i in range(nslots)]
        self.bufs = [Buf(f"{name}{i}") for i in range(nslots)]
        self.sems = [S.new_sem(f"d_{name}{i}") for i in range(nslots)]
        self.issued = 0
        self.used = 0
        self.released = 0

    def prefetch(self):
        while self.issued < len(self.plan) and self.issued < self.released + self.n:
            i = self.issued
            key, src, nk, ncols = self.plan[i]
            s = i % self.n
            self.S.dma_group("pool", self.sems[s], [(self.tiles[s][:, 0:nk, 0:ncols], src, [], [self.bufs[s]])])
            self.issued += 1

    def get(self, key):
        assert self.plan[self.used][0] == key, (self.plan[self.used][0], key)
        self.prefetch()
        assert self.used < self.issued
        s = self.used % self.n
        self.used += 1
        return self.tiles[s], self.bufs[s]

    def release(self, k=1):
        self.released += k
        assert self.released <= self.used
        self.prefetch()


def _gammas():
    return 1.0 - np.exp2(-5.0 - np.arange(H, dtype=np.float64))


def make_tables(nt):
    g = _gammas()
    inv_freq = (np.float32(10000.0) ** (-(np.arange(0, 128, 2, dtype=np.float32) / np.float32(128)))).astype(np.float32)
    rotq = np.zeros((nt, 128, H, 2, T), np.float32)
    rotk = np.zeros((nt, 128, 2, T), np.float32)
    f = np.arange(128)
    sign = np.where(f < 64, -1.0, 1.0)
    for t in range(nt):
        pos = np.zeros(T, np.float32)
        loc = np.zeros(T, np.float64)
        pos[:PT] = t * PT + np.arange(PT)
        loc[:PT] = np.arange(PT) % 128
        for s in range(4):
            for j in range(4):
                pos[PT + 4 * s + j] = PAST_LEN + j
                loc[PT + 4 * s + j] = j
        ang = (pos[None, :] * inv_freq[f % 64][:, None]).astype(np.float32)
        cos = np.cos(ang).astype(np.float32).astype(np.float64)
        sin = np.sin(ang).astype(np.float32).astype(np.float64) * sign[:, None]
        for h in range(H):
            dec = g[h] ** (loc + 1.0)
            rotq[t, :, h, 0, :] = cos * dec[None, :]
            rotq[t, :, h, 1, :] = sin * dec[None, :]
        rotk[t, :, 0, :] = cos * (128.0 ** -0.5)
        rotk[t, :, 1, :] = sin * (128.0 ** -0.5)
    e = np.arange(128)
    maskT = np.zeros((128, H, 128), np.float32)
    kdec = np.zeros((128, H), np.float32)
    for h in range(H):
        m = (e[None, :] >= e[:, None]) * (g[h] ** (-(e[:, None] + 1.0)))
        maskT[:, h, :] = m
        kdec[:, h] = g[h] ** (127.0 - e)
    smask = np.zeros((16, H, 16), np.float32)
    smd = np.zeros((16, 16), np.float32)
    for ee in range(16):
        se, je = divmod(ee, 4)
        for h in range(H):
            smd[ee, se * 4 + h] = g[h] ** (3.0 - je)
            for c in range(16):
                sc_, jc = divmod(c, 4)
                if sc_ == se and jc >= je:
                    smask[ee, h, c] = g[h] ** (-(je + 1.0))
    ident = np.eye(128, dtype=np.float32)
    return dict(rotq=rotq, rotk=rotk, maskT=maskT, kdec=kdec, smask=smask, smd=smd, ident=ident)


def build(NT=4, NL=4, dbg=None):
    gam = _gammas()
    GC = [float(gam[h] ** 128.0) for h in range(H)]
    G4 = [float(gam[h] ** 4.0) for h in range(H)]
    nc = bass.Bass("TRN2", target_bir_lowering=False)

    def din(name, shape):
        return nc.dram_tensor(name, list(shape), F32, kind="ExternalInput")

    def dout(name, shape):
        return nc.dram_tensor(name, list(shape), F32, kind="ExternalOutput")

    xp_d = din("xp", [128, 8, NT * PT])
    xs_d = din("xs", [128, 8, NT * ST])
    pp_d = din("pp", [NL, 128, 2, NT * PT])
    psm_d = din("psm", [NL, 128, 2, NT * ST])
    sr_d = din("sr", [NL, 128, NT * 16, 128])
    sc_d = din("sc", [NL, 128, 4, NT * 4, 2])
    w1i_d = din("w1i", [NL, D, 2 * DFF])
    w1o_d = din("w1o", [NL, DFF, D])
    win_d = din("win", [NL, D, WIN_EXT])
    wro_d = din("wro", [NL, 512, D])
    wco_d = din("wco", [NL, 512, D])
    wo_d = din("wo", [NL, D, D])
    w2i_d = din("w2i", [NL, D, 2 * DFF])
    w2o_d = din("w2o", [NL, DFF, D])
    wpg_d = din("wpg", [NL, D, D])
    wpl_d = din("wpl", [NL, PLE, D])
    ng_d = din("ng", [128, NL * 8 * 8])
    gn_d = din("gn", [128, NL * 4])
    cw_d = din("cw", [128, NL * 3 * 4])
    rotq_d = din("rotq", [NT, 128, H * 2 * T])
    rotk_d = din("rotk", [NT, 128, 2 * T])
    maskT_d = din("maskT", [128, H * 128])
    kdec_d = din("kdec", [128, H])
    smask_d = din("smask", [16, H * 16])
    smd_d = din("smd", [16, 16])
    ident_d = din("ident", [128, 128])

    yp_d = dout("yp", [128, 8, NT * PT])
    ys_d = dout("ys", [128, 8, NT * ST])
    retp_d = dout("retp", [NL, 128, H, 128])
    convp_d = dout("convp", [NL, 128, 4, 2])
    rets_d = dout("rets", [NL, 128, NT * 16, 128])
    convs_d = dout("convs", [NL, 128, 4, NT * 4, 2])

    S = Sched(nc)

    def sb(name, shape, dt=F32):
        return nc.alloc_sbuf_tensor("sb_" + name, list(shape), dt)

    h = sb("h", [128, 8, T]); Bhk = [Buf(f"h{k}") for k in range(8)]
    xn = sb("xn", [128, 8, T], BF16); Bxn = [Buf(f"xn{k}") for k in range(8)]
    sq = sb("sq", [128, 8, T], BF16); Bsqk = [Buf(f"sq{k}") for k in range(8)]
    y = sb("y", [128, 8, T]); By = [Buf(f"y{k}") for k in range(8)]
    arena = sb("arena", [128, NJ, T], BF16); Bar = [Buf(f"ar{j}") for j in range(NJ)]
    sgt = [sb(f"sgt{i}", [128, T]) for i in range(2)]; Bsgt = [Buf(f"sgt{i}") for i in range(2)]
    r1 = sb("r1", [128, T]); Br1 = Buf("r1")
    rstd = sb("rstd", [128, T]); Brstd = Buf("rstd")
    r1b = sb("r1b", [128, T]); Br1b = Buf("r1b")
    rstdb = sb("rstdb", [128, T]); Brstdb = Buf("rstdb")
    ones_m = sb("ones_m", [128, 128], BF16)
    ones_g = sb("ones_g", [128, 128], BF16)
    epsc = sb("epsc", [128, 1])
    dumm = sb("dumm", [128, 2]); Bdum = Buf("dumm")
    Bconst = Buf("const")
    qd = sb("qd", [128, H, T], BF16); Bqd = [Buf(f"qd{i}") for i in range(H)]
    kT = sb("kT", [128, H, T], BF16); BkT = [Buf(f"kT{i}") for i in range(H)]
    t1 = sb("t1", [128, T]); Bt1 = Buf("t1")
    t2 = sb("t2", [128, T]); Bt2 = Buf("t2")
    v_tok = sb("v_tok", [128, 4, 512], BF16); Bvt = [Buf(f"vt{i}") for i in range(4)]
    v_toks = sb("v_toks", [16, 512], BF16); Bvts = Buf("vts")
    kd_tok = sb("kd_tok", [128, 4, 512], BF16); Bkd = [Buf(f"kd{i}") for i in range(4)]
    kb = sb("kb", [16, 2, 4, 128], BF16); Bkb = [Buf(f"kb{i}") for i in range(2)]
    sg = sb("sg", [128, H, T], BF16); Bsg = [Buf(f"sg{i}") for i in range(H)]
    sTm = [sb(f"sTm{i}", [128, 128], BF16) for i in range(4)]; BsTm = [Buf(f"sTm{i}") for i in range(4)]
    sTms = sb("sTms", [16, H, 16], BF16); BsTms = [Buf(f"sTms{i}") for i in range(H)]
    Sst = sb("Sst", [128, NL, H, 128]); BSst = [[Buf(f"Sst{l}_{i}") for i in range(H)] for l in range(NL)]
    Sbf = sb("Sbf", [128, H, 128], BF16); BSbf = [Buf(f"Sbf{i}") for i in range(H)]
    S0f = sb("S0f", [128, 16, 128]); BS0f = Buf("S0f")
    S0b = sb("S0b", [128, 16, 128], BF16); BS0b = Buf("S0b")
    cc_sb = [sb(f"cc_sb{i}", [128, T]) for i in range(2)]; Bcc = [Buf(f"cc{i}") for i in range(2)]
    full_p = sb("full_p", [128, 2, PT + 2]); Bfp = [Buf(f"fp{i}") for i in range(2)]
    full_s = sb("full_s", [128, 4, 4, 6]); Bfs = Buf("fs")
    carry = sb("carry", [128, NL, 4, 2]); Bcar = [[Buf(f"car{l}_{j}") for j in range(4)] for l in range(NL)]
    c0 = sb("c0", [128, T]); Bc0 = Buf("c0")
    c1 = sb("c1", [128, T]); Bc1 = Buf("c1")
    mean_sb, Bmean = c0, Bc0
    m2, Bm2 = c1, Bc1
    ma, Bma = t1, Bt1
    mb, Bmb = t2, Bt2
    pT = sb("pT", [128, 2, T], BF16); BpT = Buf("pT")
    rotq = sb("rotq", [128, H, 2, T], BF16); rotk = sb("rotk", [128, 2, T]); Brot = Buf("rot")
    maskT = sb("maskT", [128, H, 128]); kdec = sb("kdec", [128, H])
    smask = sb("smask", [16, H, 16]); smd = sb("smd", [16, 16])
    ident = sb("ident", [128, 128], BF16)
    ng = sb("ng", [128, NL, 8, 8]); ngc = sb("ngc", [128, NL, 8, 8]); gnp = sb("gnp", [128, NL, 4]); cwp = sb("cwp", [128, NL, 3, 4])

    ps_all = nc.alloc_psum_tensor("ps_all", [128, 4096], F32)
    Bbank = [Buf(f"bank{i}") for i in range(8)]
    ctr = {"big": 0, "small": 0}

    def PV(ps):
        return ps.rearrange("p (s c) -> p s c", s=2)[:, :, 0:HT]

    def SV(ap):
        return ap.rearrange("p (s c) -> p s c", s=2)

    def big():
        s = ctr["big"] % 3
        ctr["big"] += 1
        return ps_all[:, s * 1024:(s + 1) * 1024], [Bbank[2 * s], Bbank[2 * s + 1]]

    def small():
        s = ctr["small"] % 6
        ctr["small"] += 1
        return ps_all[:, s * 512:(s + 1) * 512], [Bbank[s]]

    PSTAT = ps_all[:, 3072:4096]
    Bstat = [Bbank[6], Bbank[7]]

    for nm in ("d_const", "d_ident", "d_h", "d_rotq", "d_rotk", "d_p", "d_s0f", "d_s0b", "d_sc", "d_hout", "d_rets", "d_convs", "d_retp", "d_convp"):
        S.new_sem(nm)

    def wblk(wd, l, c0_, K, ncols=256):
        return wd.ap()[l, :, c0_:c0_ + ncols].rearrange("(k p) c -> p k c", p=128)

    planA, planB = [], []
    for t in range(NT):
        for l in range(NL):
            for b in range(11):
                planA.append((("ffi", 1, t, l, b, "g"), wblk(w1i_d, l, b * 256, D), 8, 256))
                planA.append((("ffi", 1, t, l, b, "u"), wblk(w1i_d, l, DFF + b * 256, D), 8, 256))
            for b in range(2):
                planA.append((("win", t, l, "v", b), wblk(win_d, l, OFF["v"] + b * 256, D), 8, 256))
            for nm, nb in (("q", 2), ("k", 2)):
                for b in range(nb):
                    planA.append((("win", t, l, nm, b), wblk(win_d, l, OFF[nm] + b * 256, D), 8, 256))
                    planA.append((("win", t, l, nm + "p", b), wblk(win_d, l, OFF[nm + "p"] + b * 256, D), 8, 256))
            for b in range(2):
                planA.append((("win", t, l, "g", b), wblk(win_d, l, OFF["g"] + b * 256, D), 8, 256))
            for b in range(2):
                planA.append((("win", t, l, "cc", b), wblk(win_d, l, OFF["cc"] + b * 256, D), 8, 256))
                planA.append((("win", t, l, "ch", b), wblk(win_d, l, OFF["ch"] + b * 256, D), 8, 256))
            for b in range(2):
                planA.append((("win", t, l, "cb", b), wblk(win_d, l, OFF["cb"] + b * 256, D), 8, 256))
            for nm in ("ga", "gb"):
                for b in range(4):
                    planA.append((("win", t, l, nm, b), wblk(win_d, l, OFF[nm] + b * 256, D), 8, 256))
            for b in range(4):
                planA.append((("wro", t, l, b), wblk(wro_d, l, b * 256, 512), 4, 256))
                planA.append((("wco", t, l, b), wblk(wco_d, l, b * 256, 512), 4, 256))
            for b in range(4):
                planA.append((("wo", t, l, b), wblk(wo_d, l, b * 256, D), 8, 256))
            for b in range(11):
                planA.append((("ffi", 2, t, l, b, "g"), wblk(w2i_d, l, b * 256, D), 8, 256))
                planA.append((("ffi", 2, t, l, b, "u"), wblk(w2i_d, l, DFF + b * 256, D), 8, 256))
            for b in range(4):
                planA.append((("wpg", t, l, b), wblk(wpg_d, l, b * 256, D), 8, 256))
                planA.append((("wpl", t, l, b), wblk(wpl_d, l, b * 256, PLE), 2, 256))
            for m in range(8):
                planB.append((("ffo", 1, t, l, m), wblk(w1o_d, l, m * 128, DFF, 128), NJ, 128))
            for m in range(8):
                planB.append((("ffo", 2, t, l, m), wblk(w2o_d, l, m * 128, DFF, 128), NJ, 128))
    wsA = WStream(S, nc, "wA", 6, [8, 256], planA)
    wsB = WStream(S, nc, "wB", 3, [NJ, 128], planB)

    S.op("pool", lambda e: e.memset(ones_m[:], 1.0 / 1024.0), writes=[Bconst])
    S.op("pool", lambda e: e.memset(ones_g[:], 1.0 / 128.0), writes=[Bconst])
    S.op("pool", lambda e: e.memset(epsc[:], EPS), writes=[Bconst])
    S.op("pool", lambda e: e.memset(dumm[:], 1.0), writes=[Bconst])
    S.op("pool", lambda e: e.memset(Sst[:], 0.0), writes=[b for row in BSst for b in row])
    S.op("pool", lambda e: e.memset(carry[:], 0.0), writes=[b for row in Bcar for b in row])
    S.dma_group("sp", "d_const", [
        (ng[:], ng_d.ap().rearrange("p (l n k) -> p l n k", l=NL, n=8), [], [Bconst]),
        (gnp[:], gn_d.ap().rearrange("p (l h) -> p l h", l=NL), [], [Bconst]),
        (cwp[:], cw_d.ap().rearrange("p (l w j) -> p l w j", l=NL, w=3), [], [Bconst]),
        (maskT[:], maskT_d.ap().rearrange("p (h c) -> p h c", h=H), [], [Bconst]),
        (kdec[:], kdec_d.ap(), [], [Bconst]),
        (smask[:], smask_d.ap().rearrange("p (h c) -> p h c", h=H), [], [Bconst]),
        (smd[:], smd_d.ap(), [], [Bconst]),
    ])
    S.dma_group("pool", "d_ident", [(ident[:], ident_d.ap(), [], [Bconst])])
    S.op("dve", lambda e: e.tensor_copy(ngc[:], ng[:]), reads=[Bconst], writes=[Bconst])
    for n_ in (1, 3, 5):
        S.op("dve", lambda e: e.tensor_scalar(ngc[:, :, n_, :], ng[:, :, n_, :], 0.5, None, ALU.mult), reads=[Bconst], writes=[Bconst])

    def mm(out, lhsT, rhs, start, stop, reads, writes, inc):
        S.op("pe", lambda e: e.matmul(out, lhsT, rhs, start=start, stop=stop), reads=reads, writes=writes, inc=inc)

    def proj(ps, pbufs, wfn, src, nk, wreads, kbufs):
        for k in range(nk):
            for si, (a, b, o) in enumerate(SEGS):
                mm(ps[:, o:o + HT], wfn(k), src[:, k, a:b], k == 0, k == nk - 1, wreads + [kbufs[k]], pbufs,
                   inc=(si == 1 and k == nk - 1))

    def stats(src, k0, nk, ones):
        ps, pb = big()
        for k in range(nk):
            for si, (a, b, o) in enumerate(SEGS):
                mm(ps[:, o:o + HT], ones[:], src[:, k0 + k, a:b], k == 0, k == nk - 1, [Bconst, Bsqk[k0 + k]], pb,
                   inc=(si == 1 and k == nk - 1))
        return ps, pb

    def rsqrt_into(src, bsrc, dst, bdst):
        S.op("act", lambda e: e.activation(src[:], src[:], AF.Ln), reads=[bsrc], writes=[bsrc])
        S.op("act", lambda e: e.activation(dst[:], src[:], AF.Exp, scale=-0.5), reads=[bsrc], writes=[bdst])

    def preload_ln():
        S.op("act", lambda e: e.activation(dumm[:, 1:2], dumm[:, 0:1], AF.Ln), reads=[Bconst], writes=[Bdum])

    def stat_mm(k):
        for si, (a, b, o) in enumerate(SEGS):
            mm(PSTAT[:, o:o + HT], ones_m[:], sq[:, k, a:b], k == 0, k == 7, [Bconst, Bsqk[k]], Bstat, inc=(si == 1))

    def rstd_from_stat():
        S.op("act", lambda e: e.activation(SV(r1[:]), PV(PSTAT), AF.Ln, bias=epsc[:, 0:1]), reads=Bstat + [Bconst], writes=[Br1])
        S.op("act", lambda e: e.activation(rstd[:], r1[:], AF.Exp, scale=-0.5), reads=[Br1], writes=[Brstd])

    def prenorm_stats():
        for k in range(8):
            S.op("act", lambda e: e.activation(sq[:, k, :], h[:, k, :], AF.Square), reads=[Bhk[k]], writes=[Bsqk[k]])
            if k >= 1:
                stat_mm(k - 1)
        stat_mm(7)

    def prenorm_tail(l, n):
        rstd_from_stat()
        for k in range(8):
            S.op("dve", lambda e: e.scalar_tensor_tensor(xn[:, k, :], h[:, k, :], ng[:, l, n, k:k + 1], rstd[:],
                                                          ALU.mult, ALU.mult),
                 reads=[Bhk[k], Bconst, Brstd], writes=[Bxn[k]])

    def evac_y(m, ps, pb, l, n, sq_scale):
        S.op("act", lambda e: e.activation(SV(sq[:, m, :]), PV(ps), AF.Square, scale=sq_scale), reads=pb, writes=[Bsqk[m]])
        S.op("act", lambda e: e.activation(SV(y[:, m, :]), PV(ps), AF.Copy, scale=ngc[:, l, n, m:m + 1]),
             reads=pb + [Bconst], writes=[By[m]])
        if m >= 1:
            stat_mm(m - 1)

    def post_update(l, n, coef, g_applied):
        stat_mm(7)
        rstd_from_stat()
        for k in range(8):
            if g_applied:
                S.op("dve", lambda e: e.tensor_tensor(y[:, k, :], y[:, k, :], rstd[:], ALU.mult),
                     reads=[By[k], Brstd], writes=[By[k]])
                S.op("dve", lambda e: e.tensor_tensor(h[:, k, :], h[:, k, :], y[:, k, :], ALU.add),
                     reads=[By[k], Bhk[k]], writes=[Bhk[k]])
            else:
                S.op("dve", lambda e: e.scalar_tensor_tensor(y[:, k, :], y[:, k, :], ng[:, l, n, k:k + 1], rstd[:],
                                                              ALU.mult, ALU.mult),
                     reads=[By[k], Bconst, Brstd], writes=[By[k]])
                S.op("dve", lambda e: e.scalar_tensor_tensor(h[:, k, :], y[:, k, :], float(coef), h[:, k, :], ALU.mult, ALU.add),
                     reads=[By[k], Bhk[k]], writes=[Bhk[k]])
            S.op("act", lambda e: e.activation(sq[:, k, :], h[:, k, :], AF.Square), reads=[Bhk[k]], writes=[Bsqk[k]])
            if k >= 1:
                stat_mm(k - 1)
        stat_mm(7)

    def ffn(which, t, l, npre, npost):
        prenorm_tail(l, npre)
        for b in range(11):
            gw, gwb = wsA.get(("ffi", which, t, l, b, "g"))
            uw, uwb = wsA.get(("ffi", which, t, l, b, "u"))
            for jj in range(2):
                j = 2 * b + jj
                pg, pgb = big()
                proj(pg, pgb, lambda k: gw[:, k, jj * 128:(jj + 1) * 128], xn, 8, [gwb], Bxn)
                pu, pub = big()
                proj(pu, pub, lambda k: uw[:, k, jj * 128:(jj + 1) * 128], xn, 8, [uwb], Bxn)
                st = sgt[j % 2]
                S.op("act", lambda e: e.activation(SV(st[:]), PV(pg), AF.Silu), reads=pgb, writes=[Bsgt[j % 2]])
                S.op("dve", lambda e: e.tensor_tensor(SV(arena[:, j, :]), SV(st[:]), PV(pu), ALU.mult),
                     reads=[Bsgt[j % 2]] + pub, writes=[Bar[j]])
            wsA.release(2)
        preload_ln()
        for m in range(8):
            ow, owb = wsB.get(("ffo", which, t, l, m))
            po, pob = big()
            proj(po, pob, lambda k: ow[:, k, :], arena, NJ, [owb], Bar)
            evac_y(m, po, pob, l, npost, 1.0)
            wsB.release(1)
        if dbg == "ffo":
            S.op("dve", lambda e: e.tensor_copy(h[:], y[:]), reads=By, writes=Bhk)
            return
        if dbg == "ffi":
            S.op("dve", lambda e: e.tensor_copy(h[:], arena[:, 0:8, :]), reads=Bar, writes=Bhk)
            return
        post_update(l, npost, 0.5, True)

    def mixer(t, l):
        prenorm_tail(l, 2)
        w0, w0b = wsA.get(("win", t, l, "v", 0))
        w1_, w1b = wsA.get(("win", t, l, "v", 1))
        for n in range(4):
            pv, pvb = small()
            for wi, (w, wb_) in enumerate(((w0, w0b), (w1_, w1b))):
                for k in range(8):
                    mm(pv[:, wi * 256:(wi + 1) * 256], xn[:, k, n * 128:(n + 1) * 128], w[:, k, :], k == 0, k == 7,
                       [wb_, Bxn[k]], pvb, inc=(wi == 1 and k == 7))
            S.op("act", lambda e: e.activation(v_tok[:, n, :], pv[:, 0:512], AF.Copy), reads=pvb, writes=[Bvt[n]])
        pv, pvb = small()
        for wi, (w, wb_) in enumerate(((w0, w0b), (w1_, w1b))):
            for k in range(8):
                mm(pv[0:16, wi * 256:(wi + 1) * 256], xn[:, k, PT:T], w[:, k, :], k == 0, k == 7,
                   [wb_, Bxn[k]], pvb, inc=(wi == 1 and k == 7))
        S.op("act", lambda e: e.activation(v_toks[:], pv[0:16, 0:512], AF.Copy), reads=pvb, writes=[Bvts])
        wsA.release(2)
        def k_transposes(hh):
            for n in range(4):
                tp, tpb = small()
                tpv = tp.bitcast(BF16)
                S.op("pe", lambda e: e.transpose(tpv[:, 0:128], kT[:, hh, n * 128:(n + 1) * 128], ident[:]),
                     reads=[BkT[hh], Bconst], writes=tpb)
                S.op("act", lambda e: e.activation(kd_tok[:, n, hh * 128:(hh + 1) * 128], tpv[:, 0:128], AF.Copy,
                                                   scale=kdec[:, hh:hh + 1]),
                     reads=tpb + [Bconst], writes=[Bkd[n]])

        for nm in ("q", "k"):
            for b in range(2):
                w, wb_ = wsA.get(("win", t, l, nm, b))
                wp, wpb = wsA.get(("win", t, l, nm + "p", b))
                for jj in range(2):
                    hh = 2 * b + jj
                    pa, pab = big()
                    proj(pa, pab, lambda k: w[:, k, jj * 128:(jj + 1) * 128], xn, 8, [wb_], Bxn)
                    pp_, ppb = big()
                    proj(pp_, ppb, lambda k: wp[:, k, jj * 128:(jj + 1) * 128], xn, 8, [wpb], Bxn)
                    if nm == "q":
                        cs, sn, dst, dbuf = rotq[:, hh, 0, :], rotq[:, hh, 1, :], qd, Bqd[hh]
                    else:
                        cs, sn, dst, dbuf = rotk[:, 0, :], rotk[:, 1, :], kT, BkT[hh]
                    S.op("dve", lambda e: e.tensor_tensor(SV(t1[:]), PV(pa), SV(cs), ALU.mult), reads=pab + [Brot], writes=[Bt1])
                    S.op("dve", lambda e: e.tensor_tensor(SV(t2[:]), PV(pp_), SV(sn), ALU.mult), reads=ppb + [Brot], writes=[Bt2])
                    S.op("dve", lambda e: e.tensor_tensor(dst[:, hh, :], t1[:], t2[:], ALU.add), reads=[Bt1, Bt2], writes=[dbuf])
                    if nm == "k":
                        if hh >= 1:
                            k_transposes(hh - 1)
                wsA.release(2)
        for b in range(2):
            w, wb_ = wsA.get(("win", t, l, "g", b))
            for jj in range(2):
                hh = 2 * b + jj
                pg, pgb = big()
                proj(pg, pgb, lambda k: w[:, k, jj * 128:(jj + 1) * 128], xn, 8, [wb_], Bxn)
                S.op("act", lambda e: e.activation(SV(sg[:, hh, :]), PV(pg), AF.Silu), reads=pgb, writes=[Bsg[hh]])
            wsA.release(1)
        k_transposes(H - 1)
        for b in range(2):
            cwb_, cwbb = wsA.get(("win", t, l, "cc", b))
            hwb_, hwbb = wsA.get(("win", t, l, "ch", b))
            for jj in range(2):
                j = 2 * b + jj
                pc, pcb = big()
                proj(pc, pcb, lambda k: cwb_[:, k, jj * 128:(jj + 1) * 128], xn, 8, [cwbb], Bxn)
                ccs = cc_sb[jj]
                S.op("act", lambda e: e.activation(SV(ccs[:]), PV(pc), AF.Copy), reads=pcb, writes=[Bcc[jj]])
                ph, phb = big()
                proj(ph, phb, lambda k: hwb_[:, k, jj * 128:(jj + 1) * 128], xn, 8, [hwbb], Bxn)
                S.op("act", lambda e: e.activation(SV(t1[:]), PV(ph), AF.Copy), reads=phb, writes=[Bt1])
                S.op("dve", lambda e: e.tensor_copy(full_p[:, j % 2, 0:2], carry[:, l, j, :]), reads=[Bcar[l][j]], writes=[Bfp[j % 2]])
                S.op("dve", lambda e: e.tensor_tensor(full_p[:, j % 2, 2:PT + 2], ccs[:, 0:PT], t1[:, 0:PT], ALU.mult),
                     reads=[Bcc[jj], Bt1], writes=[Bfp[j % 2]])
                S.op("dve", lambda e: e.tensor_tensor(full_s[:, j, :, 2:6],
                                                      ccs[:, PT:T].rearrange("p (s w) -> p s w", w=4),
                                                      t1[:, PT:T].rearrange("p (s w) -> p s w", w=4), ALU.mult),
                     reads=[Bcc[jj], Bt1], writes=[Bfs])
                S.op("dve", lambda e: e.tensor_copy(carry[:, l, j, :], full_p[:, j % 2, PT:PT + 2]), reads=[Bfp[j % 2]], writes=[Bcar[l][j]])
                cvo = y[:, 4 + j, :]
                S.op("dve", lambda e: e.tensor_scalar(c0[:, 0:PT], full_p[:, j % 2, 0:PT], cwp[:, l, 0, j:j + 1], None, ALU.mult),
                     reads=[Bfp[j % 2], Bconst], writes=[Bc0])
                S.op("dve", lambda e: e.scalar_tensor_tensor(c1[:, 0:PT], full_p[:, j % 2, 1:PT + 1], cwp[:, l, 1, j:j + 1], c0[:, 0:PT],
                                                              ALU.mult, ALU.add),
                     reads=[Bfp[j % 2], Bconst, Bc0], writes=[Bc1])
                S.op("dve", lambda e: e.scalar_tensor_tensor(cvo[:, 0:PT], full_p[:, j % 2, 2:PT + 2], cwp[:, l, 2, j:j + 1], c1[:, 0:PT],
                                                              ALU.mult, ALU.add),
                     reads=[Bfp[j % 2], Bconst, Bc1], writes=[By[4 + j]])
                v3 = lambda ap: ap.rearrange("p (s w) -> p s w", w=4)
                S.op("dve", lambda e: e.tensor_scalar(v3(c0[:, PT:T]), full_s[:, j, :, 0:4], cwp[:, l, 0, j:j + 1], None, ALU.mult),
                     reads=[Bfs, Bconst], writes=[Bc0])
                S.op("dve", lambda e: e.scalar_tensor_tensor(v3(c1[:, PT:T]), full_s[:, j, :, 1:5], cwp[:, l, 1, j:j + 1], v3(c0[:, PT:T]),
                                                              ALU.mult, ALU.add),
                     reads=[Bfs, Bconst, Bc0], writes=[Bc1])
                S.op("dve", lambda e: e.scalar_tensor_tensor(v3(cvo[:, PT:T]), full_s[:, j, :, 2:6], cwp[:, l, 2, j:j + 1], v3(c1[:, PT:T]),
                                                              ALU.mult, ALU.add),
                     reads=[Bfs, Bconst, Bc1], writes=[By[4 + j]])
            wsA.release(2)
        S.dma_group("sp", "d_convs", [(convs_d.ap()[l, :, :, t * 4:(t + 1) * 4, :], full_s[:, :, :, 4:6], [Bfs], [])])
        if t == NT - 1:
            S.dma_group("sp", "d_convp", [(convp_d.ap()[l], carry[:, l, :, :], Bcar[l], [])])
        for b in range(2):
            w, wb_ = wsA.get(("win", t, l, "cb", b))
            for jj in range(2):
                j = 2 * b + jj
                pcb_, pcbb = big()
                proj(pcb_, pcbb, lambda k: w[:, k, jj * 128:(jj + 1) * 128], xn, 8, [wb_], Bxn)
                S.op("dve", lambda e: e.tensor_tensor(SV(arena[:, 16 + j, :]), SV(y[:, 4 + j, :]), PV(pcb_), ALU.mult),
                     reads=[By[4 + j]] + pcbb, writes=[Bar[16 + j]])
            wsA.release(1)
        for hh in range(H):
            S.op("act", lambda e: e.activation(Sbf[:, hh, :], Sst[:, l, hh, :], AF.Copy), reads=[BSst[l][hh]], writes=[BSbf[hh]])
        for n in range(4):
            cols = slice(n * 128, (n + 1) * 128)
            hsl = [slice(hh * 128, (hh + 1) * 128) for hh in range(H)]
            pss, pus, pos = [], [], []
            for hh in range(H):
                ps_, psb = small()
                mm(ps_[:, 0:128], kT[:, hh, cols], qd[:, hh, cols], True, True, [BkT[hh], Bqd[hh]], psb, True)
                pss.append((ps_, psb))
            for hh in range(H):
                ps_, psb = pss[hh]
                S.op("dve", lambda e: e.tensor_tensor(sTm[hh][:], ps_[:, 0:128], maskT[:, hh, :], ALU.mult),
                     reads=psb + [Bconst], writes=[BsTm[hh]])
            for hh in range(H):
                po_, pob = small()
                mm(po_[:, 0:128], v_tok[:, n, hsl[hh]], sTm[hh][:], True, False, [Bvt[n], BsTm[hh]], pob, False)
                mm(po_[:, 0:128], Sbf[:, hh, :], qd[:, hh, cols], False, True, [BSbf[hh], Bqd[hh]], pob, True)
                pos.append((po_, pob))
            for hh in range(H):
                po_, pob = pos[hh]
                S.op("act", lambda e: e.activation(y[:, hh, cols], po_[:, 0:128], AF.Copy), reads=pob, writes=[By[hh]])
            for hh in range(H):
                pu_, pub = small()
                mm(pu_[:, 0:128], kd_tok[:, n, hsl[hh]], v_tok[:, n, hsl[hh]], True, True, [Bkd[n], Bvt[n]], pub, True)
                pus.append((pu_, pub))
            for hh in range(H):
                pu_, pub = pus[hh]
                S.op("dve", lambda e: e.scalar_tensor_tensor(Sst[:, l, hh, :], Sst[:, l, hh, :], GC[hh], pu_[:, 0:128],
                                                              ALU.mult, ALU.add),
                     reads=pub + [BSst[l][hh]], writes=[BSst[l][hh]])
                if n < 3:
                    S.op("act", lambda e: e.activation(Sbf[:, hh, :], Sst[:, l, hh, :], AF.Copy),
                         reads=[BSst[l][hh]], writes=[BSbf[hh]])
        if t == NT - 1:
            S.dma_group("sp", "d_retp", [(retp_d.ap()[l], Sst[:, l, :, :], BSst[l], [])])
        for hh in range(H):
            hs = slice(hh * 128, (hh + 1) * 128)
            tp, tpb = small()
            tpv = tp.bitcast(BF16)
            S.op("pe", lambda e: e.transpose(tpv[0:16, 0:128], kT[:, hh, PT:T], ident[:]),
                 reads=[BkT[hh], Bconst], writes=tpb)
            for s in range(4):
                S.op("act", lambda e: e.activation(kb[:, hh % 2, s, :], tpv[0:16, 0:128], AF.Copy,
                                                   scale=smd[:, s * 4 + hh:s * 4 + hh + 1]),
                     reads=tpb + [Bconst], writes=[Bkb[hh % 2]])
            ps_, psb = small()
            mm(ps_[0:16, 0:16], kT[:, hh, PT:T], qd[:, hh, PT:T], True, True, [BkT[hh], Bqd[hh]], psb, True)
            S.op("dve", lambda e: e.tensor_tensor(sTms[:, hh, :], ps_[0:16, 0:16], smask[:, hh, :], ALU.mult),
                 reads=psb + [Bconst], writes=[BsTms[hh]])
            po_, pob = small()
            mm(po_[:, 0:16], v_toks[:, hs], sTms[:, hh, :], True, False, [Bvts, BsTms[hh]], pob, False)
            for s in range(4):
                mm(po_[:, 4 * s:4 * s + 4], S0b[:, s * 4 + hh, :], qd[:, hh, PT + 4 * s:PT + 4 * s + 4], False, s == 3,
                   [BS0b, Bqd[hh]], pob, s == 3)
            S.op("act", lambda e: e.activation(y[:, hh, PT:T], po_[:, 0:16], AF.Copy), reads=pob, writes=[By[hh]])
            for s in range(4):
                pu_, pub = small()
                mm(pu_[:, 0:128], kb[:, hh % 2, s, :], v_toks[:, hs], True, True, [Bkb[hh % 2], Bvts], pub, True)
                S.op("dve", lambda e: e.scalar_tensor_tensor(S0f[:, s * 4 + hh, :], S0f[:, s * 4 + hh, :], G4[hh], pu_[:, 0:128],
                                                              ALU.mult, ALU.add),
                     reads=pub + [BS0f], writes=[BS0f])
        S.dma_group("sp", "d_rets", [(rets_d.ap()[l, :, t * 16:(t + 1) * 16, :], S0f[:], [BS0f], [])])
        gsets = ((c0, Bc0, c1, Bc1, r1, Br1, rstd, Brstd), (t1, Bt1, t2, Bt2, r1b, Br1b, rstdb, Brstdb))

        def gn_a1(hh):
            S.op("dve", lambda e: e.tensor_copy(sq[:, hh, :], y[:, hh, :]), reads=[By[hh]], writes=[Bsqk[hh]])
            S.op("act", lambda e: e.activation(sq[:, 4 + hh, :], y[:, hh, :], AF.Square), reads=[By[hh]], writes=[Bsqk[4 + hh]])

        def gn_a2(hh):
            mean_, bmean_, m2_, bm2_, r_, br_, rs_, brs_ = gsets[hh % 2]
            pm, pmb = stats(sq, hh, 1, ones_g)
            pq, pqb = stats(sq, 4 + hh, 1, ones_g)
            S.op("act", lambda e: e.activation(SV(mean_[:]), PV(pm), AF.Copy), reads=pmb, writes=[bmean_])
            S.op("act", lambda e: e.activation(SV(m2_[:]), PV(pm), AF.Square), reads=pmb, writes=[bm2_])
            S.op("dve", lambda e: e.scalar_tensor_tensor(SV(r_[:]), PV(pq), EPS, SV(m2_[:]), ALU.add, ALU.subtract),
                 reads=pqb + [bm2_], writes=[br_])
            rsqrt_into(r_, br_, rs_, brs_)

        def gn_b(hh):
            mean_, bmean_, m2_, bm2_, r_, br_, rs_, brs_ = gsets[hh % 2]
            S.op("dve", lambda e: e.tensor_tensor(y[:, hh, :], y[:, hh, :], mean_[:], ALU.subtract),
                 reads=[By[hh], bmean_], writes=[By[hh]])
            S.op("dve", lambda e: e.scalar_tensor_tensor(y[:, hh, :], y[:, hh, :], gnp[:, l, hh:hh + 1], rs_[:], ALU.mult, ALU.mult),
                 reads=[By[hh], Bconst, brs_], writes=[By[hh]])
            S.op("dve", lambda e: e.tensor_tensor(sg[:, hh, :], sg[:, hh, :], y[:, hh, :], ALU.mult),
                 reads=[Bsg[hh], By[hh]], writes=[Bsg[hh]])

        def gate_blk(i):
            gi, b = divmod(i, 4)
            w, wb_ = wsA.get(("win", t, l, ("ga", "gb")[gi], b))
            for jj in range(2):
                m = 2 * b + jj
                pg, pgb = big()
                proj(pg, pgb, lambda k: w[:, k, jj * 128:(jj + 1) * 128], xn, 8, [wb_], Bxn)
                S.op("dve", lambda e: e.tensor_copy(SV(arena[:, gi * 8 + m, :]), PV(pg)), reads=pgb, writes=[Bar[gi * 8 + m]])
            wsA.release(1)

        for hh in range(H):
            gn_a1(hh)
        gate_blk(0); gn_a2(0); gate_blk(1); gn_a2(1); gn_b(0); gate_blk(2); gn_a2(2); gn_b(1); gate_blk(3); gn_a2(3); gn_b(2)
        gate_blk(4); gn_b(3)
        for i in range(5, 8):
            gate_blk(i)
        for gi in range(2):
            S.op("act", lambda e: e.activation(arena[:, gi * 8:gi * 8 + 8, :], arena[:, gi * 8:gi * 8 + 8, :], AF.Tanh, scale=0.5),
                 reads=Bar[gi * 8:gi * 8 + 8], writes=Bar[gi * 8:gi * 8 + 8])
        preload_ln()
        for b in range(4):
            rw, rwb = wsA.get(("wro", t, l, b))
            cw_, cwb2 = wsA.get(("wco", t, l, b))
            for jj in range(2):
                m = 2 * b + jj
                pa, pab = big()
                proj(pa, pab, lambda k: rw[:, k, jj * 128:(jj + 1) * 128], sg, 4, [rwb], Bsg)
                pb_, pbb = big()
                proj(pb_, pbb, lambda k: cw_[:, k, jj * 128:(jj + 1) * 128], arena[:, 16:20, :], 4, [cwb2], Bar[16:20])
                S.op("dve", lambda e: e.scalar_tensor_tensor(SV(ma[:]), SV(arena[:, m, :]), 1.0, PV(pa), ALU.add, ALU.mult),
                     reads=[Bar[m]] + pab, writes=[Bma])
                S.op("dve", lambda e: e.scalar_tensor_tensor(SV(mb[:]), SV(arena[:, 8 + m, :]), 1.0, PV(pb_), ALU.add, ALU.mult),
                     reads=[Bar[8 + m]] + pbb, writes=[Bmb])
                S.op("dve", lambda e: e.tensor_tensor(xn[:, m, :], ma[:], mb[:], ALU.add), reads=[Bma, Bmb], writes=[Bxn[m]])
            wsA.release(2)
        for b in range(4):
            ww, wwb = wsA.get(("wo", t, l, b))
            for jj in range(2):
                m = 2 * b + jj
                po, pob = big()
                proj(po, pob, lambda k: ww[:, k, jj * 128:(jj + 1) * 128], xn, 8, [wwb], Bxn)
                evac_y(m, po, pob, l, 3, 0.5)
            wsA.release(1)
        post_update(l, 3, 1.0, True)

    def ple(t, l):
        prenorm_tail(l, 6)
        for b in range(4):
            gw, gwb = wsA.get(("wpg", t, l, b))
            pw, pwb = wsA.get(("wpl", t, l, b))
            for jj in range(2):
                m = 2 * b + jj
                pg, pgb = big()
                proj(pg, pgb, lambda k: gw[:, k, jj * 128:(jj + 1) * 128], xn, 8, [gwb], Bxn)
                pe_, peb = big()
                proj(pe_, peb, lambda k: pw[:, k, jj * 128:(jj + 1) * 128], pT, 2, [pwb], [BpT, BpT])
                S.op("act", lambda e: e.activation(SV(t1[:]), PV(pg), AF.Tanh, scale=0.5), reads=pgb, writes=[Bt1])
                S.op("act", lambda e: e.activation(SV(t2[:]), PV(pe_), AF.Copy, scale=0.5), reads=peb, writes=[Bt2])
                S.op("dve", lambda e: e.scalar_tensor_tensor(y[:, m, :], t1[:], 1.0, t2[:], ALU.add, ALU.mult),
                     reads=[Bt1, Bt2], writes=[By[m]])
                S.op("act", lambda e: e.activation(sq[:, m, :], y[:, m, :], AF.Square), reads=[By[m]], writes=[Bsqk[m]])
                if m >= 1:
                    stat_mm(m - 1)
            wsA.release(2)
        post_update(l, 7, 1.0, False)

    for t in range(NT):
        S.dma_group("sp", "d_h", [
            (h[:, :, 0:PT], xp_d.ap()[:, :, t * PT:(t + 1) * PT], [], Bhk),
            (h[:, :, PT:T], xs_d.ap()[:, :, t * ST:(t + 1) * ST], [], Bhk),
        ])
        prenorm_stats()
        S.dma_group("pool", "d_rotq", [
            (rotq[:], rotq_d.ap()[t].rearrange("p (h c x) -> p h c x", h=H, c=2), [], [Brot]),
        ])
        S.dma_group("sp", "d_rotk", [
            (rotk[:], rotk_d.ap()[t].rearrange("p (c x) -> p c x", c=2), [], [Brot]),
        ])
        for l in range(NL):
            S.dma_group("pool", "d_p", [
                (pT[:, :, 0:PT], pp_d.ap()[l, :, :, t * PT:(t + 1) * PT], [], [BpT]),
                (pT[:, :, PT:T], psm_d.ap()[l, :, :, t * ST:(t + 1) * ST], [], [BpT]),
            ])
            S.dma_group("sp", "d_s0f", [(S0f[:], sr_d.ap()[l, :, t * 16:(t + 1) * 16, :], [], [BS0f])])
            S.dma_group("pool", "d_s0b", [(S0b[:], sr_d.ap()[l, :, t * 16:(t + 1) * 16, :], [], [BS0b])])
            S.dma_group("sp", "d_sc", [(full_s[:, :, :, 0:2], sc_d.ap()[l, :, :, t * 4:(t + 1) * 4, :], [], [Bfs])])
            if dbg == "load":
                break
            if dbg == "pre":
                prenorm_tail(l, 0)
                S.op("dve", lambda e: e.tensor_copy(h[:], xn[:]), reads=Bxn, writes=Bhk)
                break
            ffn(1, t, l, 0, 1)
            if dbg in ("ffn1", "ffo", "ffi"):
                break
            mixer(t, l)
            if dbg == "mixer":
                break
            ffn(2, t, l, 4, 5)
            ple(t, l)
        S.dma_group("sp", "d_hout", [
            (yp_d.ap()[:, :, t * PT:(t + 1) * PT], h[:, :, 0:PT], Bhk, []),
            (ys_d.ap()[:, :, t * ST:(t + 1) * ST], h[:, :, PT:T], Bhk, []),
        ])
    assert dbg or (wsA.used == len(planA) and wsB.used == len(planB) and wsA.released == wsA.used and wsB.released == wsB.used)
    for nm in ("d_hout", "d_rets", "d_convs", "d_retp", "d_convp"):
        if S.cnt[nm] > 0:
            nc.sync.wait_ge(S.sems[nm], S.cnt[nm])
    return nc, S


def _fm(a):
    tok, F = a.shape
    return np.ascontiguousarray(a.T.reshape(F // 128, 128, tok).transpose(1, 0, 2))


def _partner_idx():
    idx = []
    for hh in range(H):
        idx += list(range(hh * 128 + 64, hh * 128 + 128)) + list(range(hh * 128, hh * 128 + 64))
    return np.array(idx)


def prepare_shared(inputs, NL=4, NT=4):
    f32 = lambda a: np.ascontiguousarray(np.asarray(a, dtype=np.float32))
    w_in = f32(inputs["w_in"])[:NL]
    pidx = _partner_idx()
    win_ext = np.ascontiguousarray(np.concatenate([w_in, w_in[:, :, pidx], w_in[:, :, 512 + pidx]], axis=2))
    ng = f32(inputs["norm_g"])[:NL]
    ng_fm = np.ascontiguousarray(ng.reshape(NL, 8, 8, 128).transpose(3, 0, 1, 2).reshape(128, NL * 64))
    gn = f32(inputs["ret_gn"])[:NL]
    gn_fm = np.ascontiguousarray(gn.reshape(NL, 4, 128).transpose(2, 0, 1).reshape(128, NL * 4))
    cw = f32(inputs["conv_w"])[:NL]
    cw_fm = np.ascontiguousarray(cw.reshape(NL, 3, 4, 128).transpose(3, 0, 1, 2).reshape(128, NL * 12))
    tb = make_tables(NT)
    sh = dict(
        w1i=f32(inputs["w_ffn1_in"])[:NL], w1o=f32(inputs["w_ffn1_out"])[:NL], win=win_ext,
        wro=f32(inputs["w_ret_out"])[:NL], wco=f32(inputs["w_conv_out"])[:NL], wo=f32(inputs["w_o"])[:NL],
        w2i=f32(inputs["w_ffn2_in"])[:NL], w2o=f32(inputs["w_ffn2_out"])[:NL],
        wpg=f32(inputs["w_ple_gate"])[:NL], wpl=f32(inputs["w_ple"])[:NL],
        ng=ng_fm, gn=gn_fm, cw=cw_fm,
        rotq=np.ascontiguousarray(tb["rotq"].reshape(NT, 128, H * 2 * T)),
        rotk=np.ascontiguousarray(tb["rotk"].reshape(NT, 128, 2 * T)),
        maskT=np.ascontiguousarray(tb["maskT"].reshape(128, H * 128)), kdec=tb["kdec"],
        smask=np.ascontiguousarray(tb["smask"].reshape(16, H * 16)), smd=tb["smd"], ident=tb["ident"],
    )
    return sh


def prepare_core(inputs, c, NL=4, NT=4):
    f32 = lambda a: np.asarray(a, dtype=np.float32)
    ns = NT * 4
    b0 = c * DEC_PER_CORE
    xp = _fm(f32(inputs["x_prompt"])[c, :NT * PT])
    xs = _fm(f32(inputs["x_sample"])[b0:b0 + ns].reshape(ns * 4, D))
    pp = np.stack([_fm(f32(inputs["p_prompt"])[l, c, :NT * PT]) for l in range(NL)])
    psm = np.stack([_fm(f32(inputs["p_sample"])[l, b0:b0 + ns].reshape(ns * 4, PLE)) for l in range(NL)])
    sr = f32(inputs["state_ret"])[:NL, b0:b0 + ns]
    sr = np.ascontiguousarray(sr.transpose(0, 3, 1, 2, 4).reshape(NL, 128, ns * 4, 128))
    sc = f32(inputs["state_conv"])[:NL, b0:b0 + ns]
    sc = np.ascontiguousarray(sc.reshape(NL, ns, 2, 4, 128).transpose(0, 4, 3, 1, 2))
    return dict(xp=xp, xs=xs, pp=pp, psm=psm, sr=sr, sc=sc)


def assemble(results, NL=4, NT=4):
    nco = len(results)
    ns = NT * 4
    yp = np.zeros((nco, NT * PT, D), np.float32)
    ys = np.zeros((nco * ns, 4, D), np.float32)
    retp = np.zeros((NL, nco, H, 128, 128), np.float32)
    convp = np.zeros((NL, nco, 2, 512), np.float32)
    rets = np.zeros((NL, nco * ns, H, 128, 128), np.float32)
    convs = np.zeros((NL, nco * ns, 2, 512), np.float32)
    for c, r in enumerate(results):
        yp[c] = np.asarray(r["yp"]).transpose(2, 1, 0).reshape(NT * PT, D)
        ys[c * ns:(c + 1) * ns] = np.asarray(r["ys"]).transpose(2, 1, 0).reshape(ns, 4, D)
        retp[:, c] = np.asarray(r["retp"]).transpose(0, 2, 1, 3)
        convp[:, c] = np.asarray(r["convp"]).transpose(0, 3, 2, 1).reshape(NL, 2, 512)
        rets[:, c * ns:(c + 1) * ns] = np.asarray(r["rets"]).reshape(NL, 128, ns, H, 128).transpose(0, 2, 3, 1, 4)
        convs[:, c * ns:(c + 1) * ns] = np.asarray(r["convs"]).transpose(0, 3, 4, 2, 1).reshape(NL, ns, 2, 512)
    return yp, ys, retp, convp, rets, convs


def kernel(**inputs):
    nc, _ = build(4, 4)
    sh = prepare_shared(inputs)
    in_maps = []
    for c in range(NCORES):
        m = dict(sh)
        m.update(prepare_core(inputs, c))
        in_maps.append(m)
    res = run_bass_kernel_spmd(nc, in_maps, core_ids=list(range(NCORES)))
    return assemble(res.results)
```

```python
import numpy as np
import concourse.bass as bass
import concourse.mybir as mybir
from concourse.bass_utils import run_bass_kernel_spmd

F32 = mybir.dt.float32
BF16 = mybir.dt.bfloat16
ALU = mybir.AluOpType
AF = mybir.ActivationFunctionType

D = 1024
DEPTH = 4
SEQ = 2048
NCORES = 8
DEC_PER_CORE = 16
DEC_SEQ = 4
PAST_LEN = 16384
H = 4
DFF = 2816
NJ = DFF // 128
PLE = 256
EPS = 1e-6
PT = 512
ST = 16
T = PT + ST
HT = T // 2
SEGS = ((0, HT, 0), (HT, T, 512))
WIN_EXT = 5632 + 1024
OFF = dict(q=0, k=512, v=1024, g=1536, cb=2048, cc=2560, ch=3072, ga=3584, gb=4608, qp=5632, kp=6144)


class Buf:
    __slots__ = ("name", "w", "r")

    def __init__(self, name):
        self.name = name
        self.w = None
        self.r = []


class Sched:
    def __init__(self, nc):
        self.nc = nc
        self.eng = {"pe": nc.tensor, "act": nc.scalar, "dve": nc.vector, "pool": nc.gpsimd, "sp": nc.sync}
        self.sems = {}
        self.cnt = {}
        self.seen = {e: {} for e in self.eng}
        for e in self.eng:
            self.sems[e] = nc.alloc_semaphore("s_" + e)
            self.cnt[e] = 0
        self.nins = 0
        self.nwaits = 0

    def new_sem(self, name):
        self.sems[name] = self.nc.alloc_semaphore(name)
        self.cnt[name] = 0
        return name

    def _wait(self, e, deps):
        best = {}
        for d in deps:
            if d is None:
                continue
            k, v = d
            if best.get(k, 0) < v:
                best[k] = v
        for k, v in best.items():
            if self.seen[e].get(k, 0) >= v:
                continue
            if k == e and (e == "pe" or v > self.cnt[e]):
                continue
            self.eng[e].wait_ge(self.sems[k], v)
            self.seen[e][k] = v
            self.nwaits += 1

    @staticmethod
    def _deps(reads, writes):
        deps = []
        for b in reads:
            deps.append(b.w)
        for b in writes:
            deps.append(b.w)
            deps.extend(b.r)
        return deps

    @staticmethod
    def _record(ev, reads, writes):
        for b in reads:
            b.r.append(ev)
            if len(b.r) > 64:
                best = {}
                for k, v in b.r:
                    if best.get(k, 0) < v:
                        best[k] = v
                b.r = list(best.items())
        for b in writes:
            b.w = ev
            b.r = []

    def _pending(self, e, deps):
        best = {}
        for d in deps:
            if d is None:
                continue
            k, v = d
            if best.get(k, 0) < v:
                best[k] = v
        out = []
        for k, v in best.items():
            if self.seen[e].get(k, 0) >= v:
                continue
            if k == e and (e == "pe" or v > self.cnt[e]):
                continue
            out.append((k, v))
        return out

    def op(self, e, fn, reads=(), writes=(), inc=True):
        deps = self._deps(reads, writes)
        emb = None
        if e in ("act", "dve", "pe"):
            pend = self._pending(e, deps)
            if pend:
                emb = pend[-1]
                deps = [d for d in deps if d is None or d[0] != emb[0]]
        self._wait(e, deps)
        ins = fn(self.eng[e])
        if emb is not None:
            ins._wait_ge(self.sems[emb[0]], emb[1])
            self.seen[e][emb[0]] = emb[1]
            self.nwaits += 1
        self.nins += 1
        if inc:
            self.cnt[e] += 1
            ins.then_inc(self.sems[e], 1)
            ev = (e, self.cnt[e])
        else:
            ev = (e, self.cnt[e] + 1)
        self._record(ev, reads, writes)
        return ev

    def dma_group(self, q, sem, items):
        deps = []
        for (_, _, rd, wr) in items:
            deps.extend(self._deps(rd, wr))
        self._wait(q, deps)
        for (o, i, _, _) in items:
            self.eng[q].dma_start(out=o, in_=i).then_inc(self.sems[sem], 16)
            self.cnt[sem] += 16
            self.nins += 1
        ev = (sem, self.cnt[sem])
        for (_, _, rd, wr) in items:
            self._record(ev, rd, wr)
        return ev


class WStream:
    def __init__(self, S, nc, name, nslots, shape, plan):
        self.S = S
        self.n = nslots
        self.plan = plan
        self.tiles = [nc.alloc_sbuf_tensor(f"{name}{i}", [128] + list(shape), BF16) for i in range(nslots)]
        self.bufs = [Buf(f"{name}{i}") for i in range(nslots)]
        self.sems = [S.new_sem(f"d_{name}{i}") for i in range(nslots)]
        self.issued = 0
        self.used = 0
        self.released = 0

    def prefetch(self):
        while self.issued < len(self.plan) and self.issued < self.released + self.n:
            i = self.issued
            key, src, nk, ncols = self.plan[i]
            s = i % self.n
            self.S.dma_group("pool", self.sems[s], [(self.tiles[s][:, 0:nk, 0:ncols], src, [], [self.bufs[s]])])
            self.issued += 1

    def get(self, key):
        assert self.plan[self.used][0] == key, (self.plan[self.used][0], key)
        self.prefetch()
        assert self.used < self.issued
        s = self.used % self.n
        self.used += 1
        return self.tiles[s], self.bufs[s]

    def release(self, k=1):
        self.released += k
        assert self.released <= self.used
        self.prefetch()


def _gammas():
    return 1.0 - np.exp2(-5.0 - np.arange(H, dtype=np.float64))


def make_tables(nt):
    g = _gammas()
    inv_freq = (np.float32(10000.0) ** (-(np.arange(0, 128, 2, dtype=np.float32) / np.float32(128)))).astype(np.float32)
    rotq = np.zeros((nt, 128, H, 2, T), np.float32)
    rotk = np.zeros((nt, 128, 2, T), np.float32)
    f = np.arange(128)
    sign = np.where(f < 64, -1.0, 1.0)
    for t in range(nt):
        pos = np.zeros(T, np.float32)
        loc = np.zeros(T, np.float64)
        pos[:PT] = t * PT + np.arange(PT)
        loc[:PT] = np.arange(PT) % 128
        for s in range(4):
            for j in range(4):
                pos[PT + 4 * s + j] = PAST_LEN + j
                loc[PT + 4 * s + j] = j
        ang = (pos[None, :] * inv_freq[f % 64][:, None]).astype(np.float32)
        cos = np.cos(ang).astype(np.float32).astype(np.float64)
        sin = np.sin(ang).astype(np.float32).astype(np.float64) * sign[:, None]
        for h in range(H):
            dec = g[h] ** (loc + 1.0)
            rotq[t, :, h, 0, :] = cos * dec[None, :]
            rotq[t, :, h, 1, :] = sin * dec[None, :]
        rotk[t, :, 0, :] = cos * (128.0 ** -0.5)
        rotk[t, :, 1, :] = sin * (128.0 ** -0.5)
    e = np.arange(128)
    maskT = np.zeros((128, H, 128), np.float32)
    kdec = np.zeros((128, H), np.float32)
    for h in range(H):
        m = (e[None, :] >= e[:, None]) * (g[h] ** (-(e[:, None] + 1.0)))
        maskT[:, h, :] = m
        kdec[:, h] = g[h] ** (127.0 - e)
    smask = np.zeros((16, H, 16), np.float32)
    smd = np.zeros((16, 16), np.float32)
    for ee in range(16):
        se, je = divmod(ee, 4)
        for h in range(H):
            smd[ee, se * 4 + h] = g[h] ** (3.0 - je)
            for c in range(16):
                sc_, jc = divmod(c, 4)
                if sc_ == se and jc >= je:
                    smask[ee, h, c] = g[h] ** (-(je + 1.0))
    ident = np.eye(128, dtype=np.float32)
    return dict(rotq=rotq, rotk=rotk, maskT=maskT, kdec=kdec, smask=smask, smd=smd, ident=ident)


def build(NT=4, NL=4, dbg=None):
    gam = _gammas()
    GC = [float(gam[h] ** 128.0) for h in range(H)]
    G4 = [float(gam[h] ** 4.0) for h in range(H)]
    nc = bass.Bass("TRN2", target_bir_lowering=False)

    def din(name, shape):
        return nc.dram_tensor(name, list(shape), F32, kind="ExternalInput")

    def dout(name, shape):
        return nc.dram_tensor(name, list(shape), F32, kind="ExternalOutput")

    xp_d = din("xp", [128, 8, NT * PT])
    xs_d = din("xs", [128, 8, NT * ST])
    pp_d = din("pp", [NL, 128, 2, NT * PT])
    psm_d = din("psm", [NL, 128, 2, NT * ST])
    sr_d = din("sr", [NL, 128, NT * 16, 128])
    sc_d = din("sc", [NL, 128, 4, NT * 4, 2])
    w1i_d = din("w1i", [NL, D, 2 * DFF])
    w1o_d = din("w1o", [NL, DFF, D])
    win_d = din("win", [NL, D, WIN_EXT])
    wro_d = din("wro", [NL, 512, D])
    wco_d = din("wco", [NL, 512, D])
    wo_d = din("wo", [NL, D, D])
    w2i_d = din("w2i", [NL, D, 2 * DFF])
    w2o_d = din("w2o", [NL, DFF, D])
    wpg_d = din("wpg", [NL, D, D])
    wpl_d = din("wpl", [NL, PLE, D])
    ng_d = din("ng", [128, NL * 8 * 8])
    gn_d = din("gn", [128, NL * 4])
    cw_d = din("cw", [128, NL * 3 * 4])
    rotq_d = din("rotq", [NT, 128, H * 2 * T])
    rotk_d = din("rotk", [NT, 128, 2 * T])
    maskT_d = din("maskT", [128, H * 128])
    kdec_d = din("kdec", [128, H])
    smask_d = din("smask", [16, H * 16])
    smd_d = din("smd", [16, 16])
    ident_d = din("ident", [128, 128])

    yp_d = dout("yp", [128, 8, NT * PT])
    ys_d = dout("ys", [128, 8, NT * ST])
    retp_d = dout("retp", [NL, 128, H, 128])
    convp_d = dout("convp", [NL, 128, 4, 2])
    rets_d = dout("rets", [NL, 128, NT * 16, 128])
    convs_d = dout("convs", [NL, 128, 4, NT * 4, 2])

    S = Sched(nc)

    def sb(name, shape, dt=F32):
        return nc.alloc_sbuf_tensor("sb_" + name, list(shape), dt)

    h = sb("h", [128, 8, T]); Bhk = [Buf(f"h{k}") for k in range(8)]
    xn = sb("xn", [128, 8, T], BF16); Bxn = [Buf(f"xn{k}") for k in range(8)]
    sq = sb("sq", [128, 8, T], BF16); Bsqk = [Buf(f"sq{k}") for k in range(8)]
    y = sb("y", [128, 8, T]); By = [Buf(f"y{k}") for k in range(8)]
    arena = sb("arena", [128, NJ, T], BF16); Bar = [Buf(f"ar{j}") for j in range(NJ)]
    sgt = [sb(f"sgt{i}", [128, T]) for i in range(2)]; Bsgt = [Buf(f"sgt{i}") for i in range(2)]
    r1 = sb("r1", [128, T]); Br1 = Buf("r1")
    rstd = sb("rstd", [128, T]); Brstd = Buf("rstd")
    r1b = sb("r1b", [128, T]); Br1b = Buf("r1b")
    rstdb = sb("rstdb", [128, T]); Brstdb = Buf("rstdb")
    ones_m = sb("ones_m", [128, 128], BF16)
    ones_g = sb("ones_g", [128, 128], BF16)
    epsc = sb("epsc", [128, 1])
    dumm = sb("dumm", [128, 2]); Bdum = Buf("dumm")
    Bconst = Buf("const")
    qd = sb("qd", [128, H, T], BF16); Bqd = [Buf(f"qd{i}") for i in range(H)]
    kT = sb("kT", [128, H, T], BF16); BkT = [Buf(f"kT{i}") for i in range(H)]
    t1 = sb("t1", [128, T]); Bt1 = Buf("t1")
    t2 = sb("t2", [128, T]); Bt2 = Buf("t2")
    v_tok = sb("v_tok", [128, 4, 512], BF16); Bvt = [Buf(f"vt{i}") for i in range(4)]
    v_toks = sb("v_toks", [16, 512], BF16); Bvts = Buf("vts")
    kd_tok = sb("kd_tok", [128, 4, 512], BF16); Bkd = [Buf(f"kd{i}") for i in range(4)]
    kb = sb("kb", [16, 2, 4, 128], BF16); Bkb = [Buf(f"kb{i}") for i in range(2)]
    sg = sb("sg", [128, H, T], BF16); Bsg = [Buf(f"sg{i}") for i in range(H)]
    sTm = [sb(f"sTm{i}", [128, 128], BF16) for i in range(4)]; BsTm = [Buf(f"sTm{i}") for i in range(4)]
    sTms = sb("sTms", [16, H, 16], BF16); BsTms = [Buf(f"sTms{i}") for i in range(H)]
    Sst = sb("Sst", [128, NL, H, 128]); BSst = [[Buf(f"Sst{l}_{i}") for i in range(H)] for l in range(NL)]
    Sbf = sb("Sbf", [128, H, 128], BF16); BSbf = [Buf(f"Sbf{i}") for i in range(H)]
    S0f = sb("S0f", [128, 16, 128]); BS0f = Buf("S0f")
    S0b = sb("S0b", [128, 16, 128], BF16); BS0b = Buf("S0b")
    cc_sb = [sb(f"cc_sb{i}", [128, T]) for i in range(2)]; Bcc = [Buf(f"cc{i}") for i in range(2)]
    full_p = sb("full_p", [128, 2, PT + 2]); Bfp = [Buf(f"fp{i}") for i in range(2)]
    full_s = sb("full_s", [128, 4, 4, 6]); Bfs = Buf("fs")
    carry = sb("carry", [128, NL, 4, 2]); Bcar = [[Buf(f"car{l}_{j}") for j in range(4)] for l in range(NL)]
    c0 = sb("c0", [128, T]); Bc0 = Buf("c0")
    c1 = sb("c1", [128, T]); Bc1 = Buf("c1")
    mean_sb, Bmean = c0, Bc0
    m2, Bm2 = c1, Bc1
    ma, Bma = t1, Bt1
    mb, Bmb = t2, Bt2
    pT = sb("pT", [128, 2, T], BF16); BpT = Buf("pT")
    rotq = sb("rotq", [128, H, 2, T], BF16); rotk = sb("rotk", [128, 2, T]); Brot = Buf("rot")
    maskT = sb("maskT", [128, H, 128]); kdec = sb("kdec", [128, H])
    smask = sb("smask", [16, H, 16]); smd = sb("smd", [16, 16])
    ident = sb("ident", [128, 128], BF16)
    ng = sb("ng", [128, NL, 8, 8]); ngc = sb("ngc", [128, NL, 8, 8]); gnp = sb("gnp", [128, NL, 4]); cwp = sb("cwp", [128, NL, 3, 4])

    ps_all = nc.alloc_psum_tensor("ps_all", [128, 4096], F32)
    Bbank = [Buf(f"bank{i}") for i in range(8)]
    ctr = {"big": 0, "small": 0}

    def PV(ps):
        return ps.rearrange("p (s c) -> p s c", s=2)[:, :, 0:HT]

    def SV(ap):
        return ap.rearrange("p (s c) -> p s c", s=2)

    def big():
        s = ctr["big"] % 3
        ctr["big"] += 1
        return ps_all[:, s * 1024:(s + 1) * 1024], [Bbank[2 * s], Bbank[2 * s + 1]]

    def small():
        s = ctr["small"] % 6
        ctr["small"] += 1
        return ps_all[:, s * 512:(s + 1) * 512], [Bbank[s]]

    PSTAT = ps_all[:, 3072:4096]
    Bstat = [Bbank[6], Bbank[7]]

    for nm in ("d_const", "d_ident", "d_h", "d_rotq", "d_rotk", "d_p", "d_s0f", "d_s0b", "d_sc", "d_hout", "d_rets", "d_convs", "d_retp", "d_convp"):
        S.new_sem(nm)

    def wblk(wd, l, c0_, K, ncols=256):
        return wd.ap()[l, :, c0_:c0_ + ncols].rearrange("(k p) c -> p k c", p=128)

    planA, planB = [], []
    for t in range(NT):
        for l in range(NL):
            for b in range(11):
                planA.append((("ffi", 1, t, l, b, "g"), wblk(w1i_d, l, b * 256, D), 8, 256))
                planA.append((("ffi", 1, t, l, b, "u"), wblk(w1i_d, l, DFF + b * 256, D), 8, 256))
            for b in range(2):
                planA.append((("win", t, l, "v", b), wblk(win_d, l, OFF["v"] + b * 256, D), 8, 256))
            for nm, nb in (("q", 2), ("k", 2)):
                for b in range(nb):
                    planA.append((("win", t, l, nm, b), wblk(win_d, l, OFF[nm] + b * 256, D), 8, 256))
                    planA.append((("win", t, l, nm + "p", b), wblk(win_d, l, OFF[nm + "p"] + b * 256, D), 8, 256))
            for b in range(2):
                planA.append((("win", t, l, "g", b), wblk(win_d, l, OFF["g"] + b * 256, D), 8, 256))
            for b in range(2):
                planA.append((("win", t, l, "cc", b), wblk(win_d, l, OFF["cc"] + b * 256, D), 8, 256))
                planA.append((("win", t, l, "ch", b), wblk(win_d, l, OFF["ch"] + b * 256, D), 8, 256))
            for b in range(2):
                planA.append((("win", t, l, "cb", b), wblk(win_d, l, OFF["cb"] + b * 256, D), 8, 256))
            for nm in ("ga", "gb"):
                for b in range(4):
                    planA.append((("win", t, l, nm, b), wblk(win_d, l, OFF[nm] + b * 256, D), 8, 256))
            for b in range(4):
                planA.append((("wro", t, l, b), wblk(wro_d, l, b * 256, 512), 4, 256))
                planA.append((("wco", t, l, b), wblk(wco_d, l, b * 256, 512), 4, 256))
            for b in range(4):
                planA.append((("wo", t, l, b), wblk(wo_d, l, b * 256, D), 8, 256))
            for b in range(11):
                planA.append((("ffi", 2, t, l, b, "g"), wblk(w2i_d, l, b * 256, D), 8, 256))
                planA.append((("ffi", 2, t, l, b, "u"), wblk(w2i_d, l, DFF + b * 256, D), 8, 256))
            for b in range(4):
                planA.append((("wpg", t, l, b), wblk(wpg_d, l, b * 256, D), 8, 256))
                planA.append((("wpl", t, l, b), wblk(wpl_d, l, b * 256, PLE), 2, 256))
            for m in range(8):
                planB.append((("ffo", 1, t, l, m), wblk(w1o_d, l, m * 128, DFF, 128), NJ, 128))
            for m in range(8):
                planB.append((("ffo", 2, t, l, m), wblk(w2o_d, l, m * 128, DFF, 128), NJ, 128))
    wsA = WStream(S, nc, "wA", 6, [8, 256], planA)
    wsB = WStream(S, nc, "wB", 3, [NJ, 128], planB)

    S.op("pool", lambda e: e.memset(ones_m[:], 1.0 / 1024.0), writes=[Bconst])
    S.op("pool", lambda e: e.memset(ones_g[:], 1.0 / 128.0), writes=[Bconst])
    S.op("pool", lambda e: e.memset(epsc[:], EPS), writes=[Bconst])
    S.op("pool", lambda e: e.memset(dumm[:], 1.0), writes=[Bconst])
    S.op("pool", lambda e: e.memset(Sst[:], 0.0), writes=[b for row in BSst for b in row])
    S.op("pool", lambda e: e.memset(carry[:], 0.0), writes=[b for row in Bcar for b in row])
    S.dma_group("sp", "d_const", [
        (ng[:], ng_d.ap().rearrange("p (l n k) -> p l n k", l=NL, n=8), [], [Bconst]),
        (gnp[:], gn_d.ap().rearrange("p (l h) -> p l h", l=NL), [], [Bconst]),
        (cwp[:], cw_d.ap().rearrange("p (l w j) -> p l w j", l=NL, w=3), [], [Bconst]),
        (maskT[:], maskT_d.ap().rearrange("p (h c) -> p h c", h=H), [], [Bconst]),
        (kdec[:], kdec_d.ap(), [], [Bconst]),
        (smask[:], smask_d.ap().rearrange("p (h c) -> p h c", h=H), [], [Bconst]),
        (smd[:], smd_d.ap(), [], [Bconst]),
    ])
    S.dma_group("pool", "d_ident", [(ident[:], ident_d.ap(), [], [Bconst])])
    S.op("dve", lambda e: e.tensor_copy(ngc[:], ng[:]), reads=[Bconst], writes=[Bconst])
    for n_ in (1, 3, 5):
        S.op("dve", lambda e: e.tensor_scalar(ngc[:, :, n_, :], ng[:, :, n_, :], 0.5, None, ALU.mult), reads=[Bconst], writes=[Bconst])

    def mm(out, lhsT, rhs, start, stop, reads, writes, inc):
        S.op("pe", lambda e: e.matmul(out, lhsT, rhs, start=start, stop=stop), reads=reads, writes=writes, inc=inc)

    def proj(ps, pbufs, wfn, src, nk, wreads, kbufs):
        for k in range(nk):
            for si, (a, b, o) in enumerate(SEGS):
                mm(ps[:, o:o + HT], wfn(k), src[:, k, a:b], k == 0, k == nk - 1, wreads + [kbufs[k]], pbufs,
                   inc=(si == 1 and k == nk - 1))

    def stats(src, k0, nk, ones):
        ps, pb = big()
        for k in range(nk):
            for si, (a, b, o) in enumerate(SEGS):
                mm(ps[:, o:o + HT], ones[:], src[:, k0 + k, a:b], k == 0, k == nk - 1, [Bconst, Bsqk[k0 + k]], pb,
                   inc=(si == 1 and k == nk - 1))
        return ps, pb

    def rsqrt_into(src, bsrc, dst, bdst):
        S.op("act", lambda e: e.activation(src[:], src[:], AF.Ln), reads=[bsrc], writes=[bsrc])
        S.op("act", lambda e: e.activation(dst[:], src[:], AF.Exp, scale=-0.5), reads=[bsrc], writes=[bdst])

    def preload_ln():
        S.op("act", lambda e: e.activation(dumm[:, 1:2], dumm[:, 0:1], AF.Ln), reads=[Bconst], writes=[Bdum])

    def stat_mm(k):
        for si, (a, b, o) in enumerate(SEGS):
            mm(PSTAT[:, o:o + HT], ones_m[:], sq[:, k, a:b], k == 0, k == 7, [Bconst, Bsqk[k]], Bstat, inc=(si == 1))

    def rstd_from_stat():
        S.op("act", lambda e: e.activation(SV(r1[:]), PV(PSTAT), AF.Ln, bias=epsc[:, 0:1]), reads=Bstat + [Bconst], writes=[Br1])
        S.op("act", lambda e: e.activation(rstd[:], r1[:], AF.Exp, scale=-0.5), reads=[Br1], writes=[Brstd])

    def prenorm_stats():
        for k in range(8):
            S.op("act", lambda e: e.activation(sq[:, k, :], h[:, k, :], AF.Square), reads=[Bhk[k]], writes=[Bsqk[k]])
            if k >= 1:
                stat_mm(k - 1)
        stat_mm(7)

    def prenorm_tail(l, n):
        rstd_from_stat()
        for k in range(8):
            S.op("dve", lambda e: e.scalar_tensor_tensor(xn[:, k, :], h[:, k, :], ng[:, l, n, k:k + 1], rstd[:],
                                                          ALU.mult, ALU.mult),
                 reads=[Bhk[k], Bconst, Brstd], writes=[Bxn[k]])

    def evac_y(m, ps, pb, l, n, sq_scale):
        S.op("act", lambda e: e.activation(SV(sq[:, m, :]), PV(ps), AF.Square, scale=sq_scale), reads=pb, writes=[Bsqk[m]])
        S.op("act", lambda e: e.activation(SV(y[:, m, :]), PV(ps), AF.Copy, scale=ngc[:, l, n, m:m + 1]),
             reads=pb + [Bconst], writes=[By[m]])
        if m >= 1:
            stat_mm(m - 1)

    def post_update(l, n, coef, g_applied):
        stat_mm(7)
        rstd_from_stat()
        for k in range(8):
            if g_applied:
                S.op("dve", lambda e: e.tensor_tensor(y[:, k, :], y[:, k, :], rstd[:], ALU.mult),
                     reads=[By[k], Brstd], writes=[By[k]])
                S.op("dve", lambda e: e.tensor_tensor(h[:, k, :], h[:, k, :], y[:, k, :], ALU.add),
                     reads=[By[k], Bhk[k]], writes=[Bhk[k]])
            else:
                S.op("dve", lambda e: e.scalar_tensor_tensor(y[:, k, :], y[:, k, :], ng[:, l, n, k:k + 1], rstd[:],
                                                              ALU.mult, ALU.mult),
                     reads=[By[k], Bconst, Brstd], writes=[By[k]])
                S.op("dve", lambda e: e.scalar_tensor_tensor(h[:, k, :], y[:, k, :], float(coef), h[:, k, :], ALU.mult, ALU.add),
                     reads=[By[k], Bhk[k]], writes=[Bhk[k]])
            S.op("act", lambda e: e.activation(sq[:, k, :], h[:, k, :], AF.Square), reads=[Bhk[k]], writes=[Bsqk[k]])
            if k >= 1:
                stat_mm(k - 1)
        stat_mm(7)

    def ffn(which, t, l, npre, npost):
        prenorm_tail(l, npre)
        for b in range(11):
            gw, gwb = wsA.get(("ffi", which, t, l, b, "g"))
            uw, uwb = wsA.get(("ffi", which, t, l, b, "u"))
            for jj in range(2):
                j = 2 * b + jj
                pg, pgb = big()
                proj(pg, pgb, lambda k: gw[:, k, jj * 128:(jj + 1) * 128], xn, 8, [gwb], Bxn)
                pu, pub = big()
                proj(pu, pub, lambda k: uw[:, k, jj * 128:(jj + 1) * 128], xn, 8, [uwb], Bxn)
                st = sgt[j % 2]
                S.op("act", lambda e: e.activation(SV(st[:]), PV(pg), AF.Silu), reads=pgb, writes=[Bsgt[j % 2]])
                S.op("dve", lambda e: e.tensor_tensor(SV(arena[:, j, :]), SV(st[:]), PV(pu), ALU.mult),
                     reads=[Bsgt[j % 2]] + pub, writes=[Bar[j]])
            wsA.release(2)
        preload_ln()
        for m in range(8):
            ow, owb = wsB.get(("ffo", which, t, l, m))
            po, pob = big()
            proj(po, pob, lambda k: ow[:, k, :], arena, NJ, [owb], Bar)
            evac_y(m, po, pob, l, npost, 1.0)
            wsB.release(1)
        if dbg == "ffo":
            S.op("dve", lambda e: e.tensor_copy(h[:], y[:]), reads=By, writes=Bhk)
            return
        if dbg == "ffi":
            S.op("dve", lambda e: e.tensor_copy(h[:], arena[:, 0:8, :]), reads=Bar, writes=Bhk)
            return
        post_update(l, npost, 0.5, True)

    def mixer(t, l):
        prenorm_tail(l, 2)
        w0, w0b = wsA.get(("win", t, l, "v", 0))
        w1_, w1b = wsA.get(("win", t, l, "v", 1))
        for n in range(4):
            pv, pvb = small()
            for wi, (w, wb_) in enumerate(((w0, w0b), (w1_, w1b))):
                for k in range(8):
                    mm(pv[:, wi * 256:(wi + 1) * 256], xn[:, k, n * 128:(n + 1) * 128], w[:, k, :], k == 0, k == 7,
                       [wb_, Bxn[k]], pvb, inc=(wi == 1 and k == 7))
            S.op("act", lambda e: e.activation(v_tok[:, n, :], pv[:, 0:512], AF.Copy), reads=pvb, writes=[Bvt[n]])
        pv, pvb = small()
        for wi, (w, wb_) in enumerate(((w0, w0b), (w1_, w1b))):
            for k in range(8):
                mm(pv[0:16, wi * 256:(wi + 1) * 256], xn[:, k, PT:T], w[:, k, :], k == 0, k == 7,
                   [wb_, Bxn[k]], pvb, inc=(wi == 1 and k == 7))
        S.op("act", lambda e: e.activation(v_toks[:], pv[0:16, 0:512], AF.Copy), reads=pvb, writes=[Bvts])
        wsA.release(2)
        def k_transposes(hh):
            for n in range(4):
                tp, tpb = small()
                tpv = tp.bitcast(BF16)
                S.op("pe", lambda e: e.transpose(tpv[:, 0:128], kT[:, hh, n * 128:(n + 1) * 128], ident[:]),
                     reads=[BkT[hh], Bconst], writes=tpb)
                S.op("act", lambda e: e.activation(kd_tok[:, n, hh * 128:(hh + 1) * 128], tpv[:, 0:128], AF.Copy,
                                                   scale=kdec[:, hh:hh + 1]),
                     reads=tpb + [Bconst], writes=[Bkd[n]])

        for nm in ("q", "k"):
            for b in range(2):
                w, wb_ = wsA.get(("win", t, l, nm, b))
                wp, wpb = wsA.get(("win", t, l, nm + "p", b))
                for jj in range(2):
                    hh = 2 * b + jj
                    pa, pab = big()
                    proj(pa, pab, lambda k: w[:, k, jj * 128:(jj + 1) * 128], xn, 8, [wb_], Bxn)
                    pp_, ppb = big()
                    proj(pp_, ppb, lambda k: wp[:, k, jj * 128:(jj + 1) * 128], xn, 8, [wpb], Bxn)
                    if nm == "q":
                        cs, sn, dst, dbuf = rotq[:, hh, 0, :], rotq[:, hh, 1, :], qd, Bqd[hh]
                    else:
                        cs, sn, dst, dbuf = rotk[:, 0, :], rotk[:, 1, :], kT, BkT[hh]
                    S.op("dve", lambda e: e.tensor_tensor(SV(t1[:]), PV(pa), SV(cs), ALU.mult), reads=pab + [Brot], writes=[Bt1])
                    S.op("dve", lambda e: e.tensor_tensor(SV(t2[:]), PV(pp_), SV(sn), ALU.mult), reads=ppb + [Brot], writes=[Bt2])
                    S.op("dve", lambda e: e.tensor_tensor(dst[:, hh, :], t1[:], t2[:], ALU.add), reads=[Bt1, Bt2], writes=[dbuf])
                    if nm == "k":
                        if hh >= 1:
                            k_transposes(hh - 1)
                wsA.release(2)
        for b in range(2):
            w, wb_ = wsA.get(("win", t, l, "g", b))
            for jj in range(2):
                hh = 2 * b + jj
                pg, pgb = big()
                proj(pg, pgb, lambda k: w[:, k, jj * 128:(jj + 1) * 128], xn, 8, [wb_], Bxn)
                S.op("act", lambda e: e.activation(SV(sg[:, hh, :]), PV(pg), AF.Silu), reads=pgb, writes=[Bsg[hh]])
            wsA.release(1)
        k_transposes(H - 1)
        for b in range(2):
            cwb_, cwbb = wsA.get(("win", t, l, "cc", b))
            hwb_, hwbb = wsA.get(("win", t, l, "ch", b))
            for jj in range(2):
                j = 2 * b + jj
                pc, pcb = big()
                proj(pc, pcb, lambda k: cwb_[:, k, jj * 128:(jj + 1) * 128], xn, 8, [cwbb], Bxn)
                ccs = cc_sb[jj]
                S.op("act", lambda e: e.activation(SV(ccs[:]), PV(pc), AF.Copy), reads=pcb, writes=[Bcc[jj]])
                ph, phb = big()
                proj(ph, phb, lambda k: hwb_[:, k, jj * 128:(jj + 1) * 128], xn, 8, [hwbb], Bxn)
                S.op("act", lambda e: e.activation(SV(t1[:]), PV(ph), AF.Copy), reads=phb, writes=[Bt1])
                S.op("dve", lambda e: e.tensor_copy(full_p[:, j % 2, 0:2], carry[:, l, j, :]), reads=[Bcar[l][j]], writes=[Bfp[j % 2]])
                S.op("dve", lambda e: e.tensor_tensor(full_p[:, j % 2, 2:PT + 2], ccs[:, 0:PT], t1[:, 0:PT], ALU.mult),
                     reads=[Bcc[jj], Bt1], writes=[Bfp[j % 2]])
                S.op("dve", lambda e: e.tensor_tensor(full_s[:, j, :, 2:6],
                                                      ccs[:, PT:T].rearrange("p (s w) -> p s w", w=4),
                                                      t1[:, PT:T].rearrange("p (s w) -> p s w", w=4), ALU.mult),
                     reads=[Bcc[jj], Bt1], writes=[Bfs])
                S.op("dve", lambda e: e.tensor_copy(carry[:, l, j, :], full_p[:, j % 2, PT:PT + 2]), reads=[Bfp[j % 2]], writes=[Bcar[l][j]])
                cvo = y[:, 4 + j, :]
                S.op("dve", lambda e: e.tensor_scalar(c0[:, 0:PT], full_p[:, j % 2, 0:PT], cwp[:, l, 0, j:j + 1], None, ALU.mult),
                     reads=[Bfp[j % 2], Bconst], writes=[Bc0])
                S.op("dve", lambda e: e.scalar_tensor_tensor(c1[:, 0:PT], full_p[:, j % 2, 1:PT + 1], cwp[:, l, 1, j:j + 1], c0[:, 0:PT],
                                                              ALU.mult, ALU.add),
                     reads=[Bfp[j % 2], Bconst, Bc0], writes=[Bc1])
                S.op("dve", lambda e: e.scalar_tensor_tensor(cvo[:, 0:PT], full_p[:, j % 2, 2:PT + 2], cwp[:, l, 2, j:j + 1], c1[:, 0:PT],
                                                              ALU.mult, ALU.add),
                     reads=[Bfp[j % 2], Bconst, Bc1], writes=[By[4 + j]])
                v3 = lambda ap: ap.rearrange("p (s w) -> p s w", w=4)
                S.op("dve", lambda e: e.tensor_scalar(v3(c0[:, PT:T]), full_s[:, j, :, 0:4], cwp[:, l, 0, j:j + 1], None, ALU.mult),
                     reads=[Bfs, Bconst], writes=[Bc0])
                S.op("dve", lambda e: e.scalar_tensor_tensor(v3(c1[:, PT:T]), full_s[:, j, :, 1:5], cwp[:, l, 1, j:j + 1], v3(c0[:, PT:T]),
                                                              ALU.mult, ALU.add),
                     reads=[Bfs, Bconst, Bc0], writes=[Bc1])
                S.op("dve", lambda e: e.scalar_tensor_tensor(v3(cvo[:, PT:T]), full_s[:, j, :, 2:6], cwp[:, l, 2, j:j + 1], v3(c1[:, PT:T]),
                                                              ALU.mult, ALU.add),
                     reads=[Bfs, Bconst, Bc1], writes=[By[4 + j]])
            wsA.release(2)
        S.dma_group("sp", "d_convs", [(convs_d.ap()[l, :, :, t * 4:(t + 1) * 4, :], full_s[:, :, :, 4:6], [Bfs], [])])
        if t == NT - 1:
            S.dma_group("sp", "d_convp", [(convp_d.ap()[l], carry[:, l, :, :], Bcar[l], [])])
        for b in range(2):
            w, wb_ = wsA.get(("win", t, l, "cb", b))
            for jj in range(2):
                j = 2 * b + jj
                pcb_, pcbb = big()
                proj(pcb_, pcbb, lambda k: w[:, k, jj * 128:(jj + 1) * 128], xn, 8, [wb_], Bxn)
                S.op("dve", lambda e: e.tensor_tensor(SV(arena[:, 16 + j, :]), SV(y[:, 4 + j, :]), PV(pcb_), ALU.mult),
                     reads=[By[4 + j]] + pcbb, writes=[Bar[16 + j]])
            wsA.release(1)
        for hh in range(H):
            S.op("act", lambda e: e.activation(Sbf[:, hh, :], Sst[:, l, hh, :], AF.Copy), reads=[BSst[l][hh]], writes=[BSbf[hh]])
        for n in range(4):
            cols = slice(n * 128, (n + 1) * 128)
            hsl = [slice(hh * 128, (hh + 1) * 128) for hh in range(H)]
            pss, pus, pos = [], [], []
            for hh in range(H):
                ps_, psb = small()
                mm(ps_[:, 0:128], kT[:, hh, cols], qd[:, hh, cols], True, True, [BkT[hh], Bqd[hh]], psb, True)
                pss.append((ps_, psb))
            for hh in range(H):
                ps_, psb = pss[hh]
                S.op("dve", lambda e: e.tensor_tensor(sTm[hh][:], ps_[:, 0:128], maskT[:, hh, :], ALU.mult),
                     reads=psb + [Bconst], writes=[BsTm[hh]])
            for hh in range(H):
                po_, pob = small()
                mm(po_[:, 0:128], v_tok[:, n, hsl[hh]], sTm[hh][:], True, False, [Bvt[n], BsTm[hh]], pob, False)
                mm(po_[:, 0:128], Sbf[:, hh, :], qd[:, hh, cols], False, True, [BSbf[hh], Bqd[hh]], pob, True)
                pos.append((po_, pob))
            for hh in range(H):
                po_, pob = pos[hh]
                S.op("act", lambda e: e.activation(y[:, hh, cols], po_[:, 0:128], AF.Copy), reads=pob, writes=[By[hh]])
            for hh in range(H):
                pu_, pub = small()
                mm(pu_[:, 0:128], kd_tok[:, n, hsl[hh]], v_tok[:, n, hsl[hh]], True, True, [Bkd[n], Bvt[n]], pub, True)
                pus.append((pu_, pub))
            for hh in range(H):
                pu_, pub = pus[hh]
                S.op("dve", lambda e: e.scalar_tensor_tensor(Sst[:, l, hh, :], Sst[:, l, hh, :], GC[hh], pu_[:, 0:128],
                                                              ALU.mult, ALU.add),
                     reads=pub + [BSst[l][hh]], writes=[BSst[l][hh]])
                if n < 3:
                    S.op("act", lambda e: e.activation(Sbf[:, hh, :], Sst[:, l, hh, :], AF.Copy),
                         reads=[BSst[l][hh]], writes=[BSbf[hh]])
        if t == NT - 1:
            S.dma_group("sp", "d_retp", [(retp_d.ap()[l], Sst[:, l, :, :], BSst[l], [])])
        for hh in range(H):
            hs = slice(hh * 128, (hh + 1) * 128)
            tp, tpb = small()
            tpv = tp.bitcast(BF16)
            S.op("pe", lambda e: e.transpose(tpv[0:16, 0:128], kT[:, hh, PT:T], ident[:]),
                 reads=[BkT[hh], Bconst], writes=tpb)
            for s in range(4):
                S.op("act", lambda e: e.activation(kb[:, hh % 2, s, :], tpv[0:16, 0:128], AF.Copy,
                                                   scale=smd[:, s * 4 + hh:s * 4 + hh + 1]),
                     reads=tpb + [Bconst], writes=[Bkb[hh % 2]])
            ps_, psb = small()
            mm(ps_[0:16, 0:16], kT[:, hh, PT:T], qd[:, hh, PT:T], True, True, [BkT[hh], Bqd[hh]], psb, True)
            S.op("dve", lambda e: e.tensor_tensor(sTms[:, hh, :], ps_[0:16, 0:16], smask[:, hh, :], ALU.mult),
                 reads=psb + [Bconst], writes=[BsTms[hh]])
            po_, pob = small()
            mm(po_[:, 0:16], v_toks[:, hs], sTms[:, hh, :], True, False, [Bvts, BsTms[hh]], pob, False)
            for s in range(4):
                mm(po_[:, 4 * s:4 * s + 4], S0b[:, s * 4 + hh, :], qd[:, hh, PT + 4 * s:PT + 4 * s + 4], False, s == 3,
                   [BS0b, Bqd[hh]], pob, s == 3)
            S.op("act", lambda e: e.activation(y[:, hh, PT:T], po_[:, 0:16], AF.Copy), reads=pob, writes=[By[hh]])
            for s in range(4):
                pu_, pub = small()
                mm(pu_[:, 0:128], kb[:, hh % 2, s, :], v_toks[:, hs], True, True, [Bkb[hh % 2], Bvts], pub, True)
                S.op("dve", lambda e: e.scalar_tensor_tensor(S0f[:, s * 4 + hh, :], S0f[:, s * 4 + hh, :], G4[hh], pu_[:, 0:128],
                                                              ALU.mult, ALU.add),
                     reads=pub + [BS0f], writes=[BS0f])
        S.dma_group("sp", "d_rets", [(rets_d.ap()[l, :, t * 16:(t + 1) * 16, :], S0f[:], [BS0f], [])])
        gsets = ((c0, Bc0, c1, Bc1, r1, Br1, rstd, Brstd), (t1, Bt1, t2, Bt2, r1b, Br1b, rstdb, Brstdb))

        def gn_a1(hh):
            S.op("dve", lambda e: e.tensor_copy(sq[:, hh, :], y[:, hh, :]), reads=[By[hh]], writes=[Bsqk[hh]])
            S.op("act", lambda e: e.activation(sq[:, 4 + hh, :], y[:, hh, :], AF.Square), reads=[By[hh]], writes=[Bsqk[4 + hh]])

        def gn_a2(hh):
            mean_, bmean_, m2_, bm2_, r_, br_, rs_, brs_ = gsets[hh % 2]
            pm, pmb = stats(sq, hh, 1, ones_g)
            pq, pqb = stats(sq, 4 + hh, 1, ones_g)
            S.op("act", lambda e: e.activation(SV(mean_[:]), PV(pm), AF.Copy), reads=pmb, writes=[bmean_])
            S.op("act", lambda e: e.activation(SV(m2_[:]), PV(pm), AF.Square), reads=pmb, writes=[bm2_])
            S.op("dve", lambda e: e.scalar_tensor_tensor(SV(r_[:]), PV(pq), EPS, SV(m2_[:]), ALU.add, ALU.subtract),
                 reads=pqb + [bm2_], writes=[br_])
            rsqrt_into(r_, br_, rs_, brs_)

        def gn_b(hh):
            mean_, bmean_, m2_, bm2_, r_, br_, rs_, brs_ = gsets[hh % 2]
            S.op("dve", lambda e: e.tensor_tensor(y[:, hh, :], y[:, hh, :], mean_[:], ALU.subtract),
                 reads=[By[hh], bmean_], writes=[By[hh]])
            S.op("dve", lambda e: e.scalar_tensor_tensor(y[:, hh, :], y[:, hh, :], gnp[:, l, hh:hh + 1], rs_[:], ALU.mult, ALU.mult),
                 reads=[By[hh], Bconst, brs_], writes=[By[hh]])
            S.op("dve", lambda e: e.tensor_tensor(sg[:, hh, :], sg[:, hh, :], y[:, hh, :], ALU.mult),
                 reads=[Bsg[hh], By[hh]], writes=[Bsg[hh]])

        def gate_blk(i):
            gi, b = divmod(i, 4)
            w, wb_ = wsA.get(("win", t, l, ("ga", "gb")[gi], b))
            for jj in range(2):
                m = 2 * b + jj
                pg, pgb = big()
                proj(pg, pgb, lambda k: w[:, k, jj * 128:(jj + 1) * 128], xn, 8, [wb_], Bxn)
                S.op("dve", lambda e: e.tensor_copy(SV(arena[:, gi * 8 + m, :]), PV(pg)), reads=pgb, writes=[Bar[gi * 8 + m]])
            wsA.release(1)

        for hh in range(H):
            gn_a1(hh)
        gate_blk(0); gn_a2(0); gate_blk(1); gn_a2(1); gn_b(0); gate_blk(2); gn_a2(2); gn_b(1); gate_blk(3); gn_a2(3); gn_b(2)
        gate_blk(4); gn_b(3)
        for i in range(5, 8):
            gate_blk(i)
        for gi in range(2):
            S.op("act", lambda e: e.activation(arena[:, gi * 8:gi * 8 + 8, :], arena[:, gi * 8:gi * 8 + 8, :], AF.Tanh, scale=0.5),
                 reads=Bar[gi * 8:gi * 8 + 8], writes=Bar[gi * 8:gi * 8 + 8])
        preload_ln()
        for b in range(4):
            rw, rwb = wsA.get(("wro", t, l, b))
            cw_, cwb2 = wsA.get(("wco", t, l, b))
            for jj in range(2):
                m = 2 * b + jj
                pa, pab = big()
                proj(pa, pab, lambda k: rw[:, k, jj * 128:(jj + 1) * 128], sg, 4, [rwb], Bsg)
                pb_, pbb = big()
                proj(pb_, pbb, lambda k: cw_[:, k, jj * 128:(jj + 1) * 128], arena[:, 16:20, :], 4, [cwb2], Bar[16:20])
                S.op("dve", lambda e: e.scalar_tensor_tensor(SV(ma[:]), SV(arena[:, m, :]), 1.0, PV(pa), ALU.add, ALU.mult),
                     reads=[Bar[m]] + pab, writes=[Bma])
                S.op("dve", lambda e: e.scalar_tensor_tensor(SV(mb[:]), SV(arena[:, 8 + m, :]), 1.0, PV(pb_), ALU.add, ALU.mult),
                     reads=[Bar[8 + m]] + pbb, writes=[Bmb])
                S.op("dve", lambda e: e.tensor_tensor(xn[:, m, :], ma[:], mb[:], ALU.add), reads=[Bma, Bmb], writes=[Bxn[m]])
            wsA.release(2)
        for b in range(4):
            ww, wwb = wsA.get(("wo", t, l, b))
            for jj in range(2):
                m = 2 * b + jj
                po, pob = big()
                proj(po, pob, lambda k: ww[:, k, jj * 128:(jj + 1) * 128], xn, 8, [wwb], Bxn)
                evac_y(m, po, pob, l, 3, 0.5)
            wsA.release(1)
        post_update(l, 3, 1.0, True)

    def ple(t, l):
        prenorm_tail(l, 6)
        for b in range(4):
            gw, gwb = wsA.get(("wpg", t, l, b))
            pw, pwb = wsA.get(("wpl", t, l, b))
            for jj in range(2):
                m = 2 * b + jj
                pg, pgb = big()
                proj(pg, pgb, lambda k: gw[:, k, jj * 128:(jj + 1) * 128], xn, 8, [gwb], Bxn)
                pe_, peb = big()
                proj(pe_, peb, lambda k: pw[:, k, jj * 128:(jj + 1) * 128], pT, 2, [pwb], [BpT, BpT])
                S.op("act", lambda e: e.activation(SV(t1[:]), PV(pg), AF.Tanh, scale=0.5), reads=pgb, writes=[Bt1])
                S.op("act", lambda e: e.activation(SV(t2[:]), PV(pe_), AF.Copy, scale=0.5), reads=peb, writes=[Bt2])
                S.op("dve", lambda e: e.scalar_tensor_tensor(y[:, m, :], t1[:], 1.0, t2[:], ALU.add, ALU.mult),
                     reads=[Bt1, Bt2], writes=[By[m]])
                S.op("act", lambda e: e.activation(sq[:, m, :], y[:, m, :], AF.Square), reads=[By[m]], writes=[Bsqk[m]])
                if m >= 1:
                    stat_mm(m - 1)
            wsA.release(2)
        post_update(l, 7, 1.0, False)

    for t in range(NT):
        S.dma_group("sp", "d_h", [
            (h[:, :, 0:PT], xp_d.ap()[:, :, t * PT:(t + 1) * PT], [], Bhk),
            (h[:, :, PT:T], xs_d.ap()[:, :, t * ST:(t + 1) * ST], [], Bhk),
        ])
        prenorm_stats()
        S.dma_group("pool", "d_rotq", [
            (rotq[:], rotq_d.ap()[t].rearrange("p (h c x) -> p h c x", h=H, c=2), [], [Brot]),
        ])
        S.dma_group("sp", "d_rotk", [
            (rotk[:], rotk_d.ap()[t].rearrange("p (c x) -> p c x", c=2), [], [Brot]),
        ])
        for l in range(NL):
            S.dma_group("pool", "d_p", [
                (pT[:, :, 0:PT], pp_d.ap()[l, :, :, t * PT:(t + 1) * PT], [], [BpT]),
                (pT[:, :, PT:T], psm_d.ap()[l, :, :, t * ST:(t + 1) * ST], [], [BpT]),
            ])
            S.dma_group("sp", "d_s0f", [(S0f[:], sr_d.ap()[l, :, t * 16:(t + 1) * 16, :], [], [BS0f])])
            S.dma_group("pool", "d_s0b", [(S0b[:], sr_d.ap()[l, :, t * 16:(t + 1) * 16, :], [], [BS0b])])
            S.dma_group("sp", "d_sc", [(full_s[:, :, :, 0:2], sc_d.ap()[l, :, :, t * 4:(t + 1) * 4, :], [], [Bfs])])
            if dbg == "load":
                break
            if dbg == "pre":
                prenorm_tail(l, 0)
                S.op("dve", lambda e: e.tensor_copy(h[:], xn[:]), reads=Bxn, writes=Bhk)
                break
            ffn(1, t, l, 0, 1)
            if dbg in ("ffn1", "ffo", "ffi"):
                break
            mixer(t, l)
            if dbg == "mixer":
                break
            ffn(2, t, l, 4, 5)
            ple(t, l)
        S.dma_group("sp", "d_hout", [
            (yp_d.ap()[:, :, t * PT:(t + 1) * PT], h[:, :, 0:PT], Bhk, []),
            (ys_d.ap()[:, :, t * ST:(t + 1) * ST], h[:, :, PT:T], Bhk, []),
        ])
    assert dbg or (wsA.used == len(planA) and wsB.used == len(planB) and wsA.released == wsA.used and wsB.released == wsB.used)
    for nm in ("d_hout", "d_rets", "d_convs", "d_retp", "d_convp"):
        if S.cnt[nm] > 0:
            nc.sync.wait_ge(S.sems[nm], S.cnt[nm])
    return nc, S


def _fm(a):
    tok, F = a.shape
    return np.ascontiguousarray(a.T.reshape(F // 128, 128, tok).transpose(1, 0, 2))


def _partner_idx():
    idx = []
    for hh in range(H):
        idx += list(range(hh * 128 + 64, hh * 128 + 128)) + list(range(hh * 128, hh * 128 + 64))
    return np.array(idx)


def prepare_shared(inputs, NL=4, NT=4):
    f32 = lambda a: np.ascontiguousarray(np.asarray(a, dtype=np.float32))
    w_in = f32(inputs["w_in"])[:NL]
    pidx = _partner_idx()
    win_ext = np.ascontiguousarray(np.concatenate([w_in, w_in[:, :, pidx], w_in[:, :, 512 + pidx]], axis=2))
    ng = f32(inputs["norm_g"])[:NL]
    ng_fm = np.ascontiguousarray(ng.reshape(NL, 8, 8, 128).transpose(3, 0, 1, 2).reshape(128, NL * 64))
    gn = f32(inputs["ret_gn"])[:NL]
    gn_fm = np.ascontiguousarray(gn.reshape(NL, 4, 128).transpose(2, 0, 1).reshape(128, NL * 4))
    cw = f32(inputs["conv_w"])[:NL]
    cw_fm = np.ascontiguousarray(cw.reshape(NL, 3, 4, 128).transpose(3, 0, 1, 2).reshape(128, NL * 12))
    tb = make_tables(NT)
    sh = dict(
        w1i=f32(inputs["w_ffn1_in"])[:NL], w1o=f32(inputs["w_ffn1_out"])[:NL], win=win_ext,
        wro=f32(inputs["w_ret_out"])[:NL], wco=f32(inputs["w_conv_out"])[:NL], wo=f32(inputs["w_o"])[:NL],
        w2i=f32(inputs["w_ffn2_in"])[:NL], w2o=f32(inputs["w_ffn2_out"])[:NL],
        wpg=f32(inputs["w_ple_gate"])[:NL], wpl=f32(inputs["w_ple"])[:NL],
        ng=ng_fm, gn=gn_fm, cw=cw_fm,
        rotq=np.ascontiguousarray(tb["rotq"].reshape(NT, 128, H * 2 * T)),
        rotk=np.ascontiguousarray(tb["rotk"].reshape(NT, 128, 2 * T)),
        maskT=np.ascontiguousarray(tb["maskT"].reshape(128, H * 128)), kdec=tb["kdec"],
        smask=np.ascontiguousarray(tb["smask"].reshape(16, H * 16)), smd=tb["smd"], ident=tb["ident"],
    )
    return sh


def prepare_core(inputs, c, NL=4, NT=4):
    f32 = lambda a: np.asarray(a, dtype=np.float32)
    ns = NT * 4
    b0 = c * DEC_PER_CORE
    xp = _fm(f32(inputs["x_prompt"])[c, :NT * PT])
    xs = _fm(f32(inputs["x_sample"])[b0:b0 + ns].reshape(ns * 4, D))
    pp = np.stack([_fm(f32(inputs["p_prompt"])[l, c, :NT * PT]) for l in range(NL)])
    psm = np.stack([_fm(f32(inputs["p_sample"])[l, b0:b0 + ns].reshape(ns * 4, PLE)) for l in range(NL)])
    sr = f32(inputs["state_ret"])[:NL, b0:b0 + ns]
    sr = np.ascontiguousarray(sr.transpose(0, 3, 1, 2, 4).reshape(NL, 128, ns * 4, 128))
    sc = f32(inputs["state_conv"])[:NL, b0:b0 + ns]
    sc = np.ascontiguousarray(sc.reshape(NL, ns, 2, 4, 128).transpose(0, 4, 3, 1, 2))
    return dict(xp=xp, xs=xs, pp=pp, psm=psm, sr=sr, sc=sc)


def assemble(results, NL=4, NT=4):
    nco = len(results)
    ns = NT * 4
    yp = np.zeros((nco, NT * PT, D), np.float32)
    ys = np.zeros((nco * ns, 4, D), np.float32)
    retp = np.zeros((NL, nco, H, 128, 128), np.float32)
    convp = np.zeros((NL, nco, 2, 512), np.float32)
    rets = np.zeros((NL, nco * ns, H, 128, 128), np.float32)
    convs = np.zeros((NL, nco * ns, 2, 512), np.float32)
    for c, r in enumerate(results):
        yp[c] = np.asarray(r["yp"]).transpose(2, 1, 0).reshape(NT * PT, D)
        ys[c * ns:(c + 1) * ns] = np.asarray(r["ys"]).transpose(2, 1, 0).reshape(ns, 4, D)
        retp[:, c] = np.asarray(r["retp"]).transpose(0, 2, 1, 3)
        convp[:, c] = np.asarray(r["convp"]).transpose(0, 3, 2, 1).reshape(NL, 2, 512)
        rets[:, c * ns:(c + 1) * ns] = np.asarray(r["rets"]).reshape(NL, 128, ns, H, 128).transpose(0, 2, 3, 1, 4)
        convs[:, c * ns:(c + 1) * ns] = np.asarray(r["convs"]).transpose(0, 3, 4, 2, 1).reshape(NL, ns, 2, 512)
    return yp, ys, retp, convp, rets, convs


def kernel(**inputs):
    nc, _ = build(4, 4)
    sh = prepare_shared(inputs)
    in_maps = []
    for c in range(NCORES):
        m = dict(sh)
        m.update(prepare_core(inputs, c))
        in_maps.append(m)
    res = run_bass_kernel_spmd(nc, in_maps, core_ids=list(range(NCORES)))
    return assemble(res.results)
```
